# Optimizing a Trainium2 kernel written in Bass

```python
import jax, jax.numpy as jnp
from jax import lax
import numpy as np


D_MODEL = 2048
BATCH = 4
SEQ = 2048
DEPTH = 4

GRID_W = 64
CTX_LEN = 256
F32 = jnp.float32

N_BRANCH = 4
RMS_EPS = 1e-6
CONV_DIM = 512
CONV_WIDTH = 31
LN_EPS = 1e-5
SSD_HEADS = 12
SSD_HEAD_DIM = 64
SSD_DIM = SSD_HEADS * SSD_HEAD_DIM
SSD_GROUPS = 4
SSD_STATE = 128
SSD_CONV = 5
SSD_CHUNK = 128
SSD_XBC = SSD_DIM + 2 * SSD_GROUPS * SSD_STATE
SSD_IN = SSD_DIM + SSD_XBC + 2 * SSD_HEADS
FOURIER_GROUPS = 4
FOURIER_GROUP_DIM = 128
FOURIER_DIM = FOURIER_GROUPS * FOURIER_GROUP_DIM
RWKV_HEADS = 8
RWKV_HEAD_DIM = 64
RWKV_DIM = RWKV_HEADS * RWKV_HEAD_DIM
DECAY_LORA = 64
ICL_LORA = 64
GATE_LORA = 128
GN_EPS = 64e-5
RWKV_IN = 3 * RWKV_DIM + 2 * DECAY_LORA + ICL_LORA + GATE_LORA
RWKV_OFFSETS = [RWKV_DIM, 2 * RWKV_DIM, 3 * RWKV_DIM, 3 * RWKV_DIM + DECAY_LORA, 3 * RWKV_DIM + 2 * DECAY_LORA, 3 * RWKV_DIM + 2 * DECAY_LORA + ICL_LORA]
REC_IN = SSD_IN + RWKV_IN
IN_OFFSETS = [SSD_IN, REC_IN, REC_IN + 2 * CONV_DIM, REC_IN + 2 * CONV_DIM + FOURIER_DIM]
IN_DIM = REC_IN + 2 * CONV_DIM + FOURIER_DIM + N_BRANCH * D_MODEL
D_FF = 4 * D_MODEL

kernel_name = 'hybrid_gated_conv_ssd_fourier_rwkv_dit'


def rms_norm(v, g):
    vf = v.astype(F32)
    y = vf * lax.rsqrt(jnp.mean(vf * vf, axis=-1, keepdims=True) + RMS_EPS)
    return (y * g.astype(F32)).astype(v.dtype)


def layer_norm(v, g, b, eps):
    vf = v.astype(F32)
    mu = jnp.mean(vf, axis=-1, keepdims=True)
    var = jnp.mean(jnp.square(vf - mu), axis=-1, keepdims=True)
    return ((vf - mu) * lax.rsqrt(var + eps) * g.astype(F32) + b.astype(F32)).astype(v.dtype)


def _flip(t):
    return jnp.flip(t, axis=1)


def dwconv1d(v, w, bias):
    k, ch = w.shape
    y = lax.conv_general_dilated(v, w.astype(v.dtype)[:, None, :], (1,), [((k - 1) // 2, k // 2)],
                                 dimension_numbers=('NWC', 'WIO', 'NWC'), feature_group_count=ch)
    return y + bias.astype(v.dtype)


def token_shift(v):
    z = jnp.zeros_like(v[:, :1])
    prev = jnp.concatenate([z, v[:, :-1]], axis=1)
    nxt = jnp.concatenate([v[:, 1:], z], axis=1)
    return 0.5 * (prev + nxt) - v


def to_cols(t, rows):
    b, l, ch = t.shape
    return t.reshape(b, rows, GRID_W, ch).transpose(0, 2, 1, 3).reshape(b, l, ch)


def to_rows(t, rows):
    b, l, ch = t.shape
    return t.reshape(b, GRID_W, rows, ch).transpose(0, 2, 1, 3).reshape(b, l, ch)


def segsum(a):
    t = a.shape[-1]
    cs = jnp.cumsum(a, axis=-1)
    diff = cs[..., :, None] - cs[..., None, :]
    return jnp.where(jnp.tril(jnp.ones((t, t), dtype=bool)), diff, -jnp.inf)


def ssd_scan(xs, dt, A, Bm, Cm, h0, need_y):
    b, l, nh, p = xs.shape
    g, n = Bm.shape[2], Bm.shape[3]
    r = nh // g
    c, q = l // SSD_CHUNK, SSD_CHUNK
    xd = (xs * dt[..., None]).reshape(b, c, q, g, r, p)
    a = (dt * A).reshape(b, c, q, g, r)
    a_cs = jnp.cumsum(a, axis=2)
    Bc = Bm.reshape(b, c, q, g, n)
    decay_s = jnp.exp(a_cs[:, :, -1:] - a_cs)
    states = jnp.einsum('bcsgn,bcsgrp->bcgrpn', Bc, xd * decay_s[..., None])
    states = jnp.concatenate([h0.reshape(b, 1, g, r, p, n), states], axis=1)
    chunk_tot = jnp.pad(a_cs[:, :, -1], ((0, 0), (1, 0), (0, 0), (0, 0))).transpose(0, 2, 3, 1)
    decay_c = jnp.exp(segsum(chunk_tot))
    states = jnp.einsum('bgrzc,bcgrpn->bzgrpn', decay_c, states)
    final = states[:, -1].reshape(b, nh, p, n)
    if not need_y:
        return None, final
    Cc = Cm.reshape(b, c, q, g, n)
    L = jnp.exp(segsum(a.transpose(0, 3, 4, 1, 2)))
    CB = jnp.einsum('bclgn,bcsgn->bgcls', Cc, Bc)
    y_diag = jnp.einsum('bgrcls,bcsgrp->bclgrp', CB[:, :, None] * L, xd)
    y_off = jnp.einsum('bclgn,bcgrpn->bclgrp', Cc, states[:, :-1]) * jnp.exp(a_cs)[..., None]
    return (y_diag + y_off).reshape(b, l, nh, p), final


def ssd_branch(u, uc, ctx_out, conv_w, conv_b, A_log, dt_bias, D_skip, norm_g, w_out):
    A = -jnp.exp(A_log.astype(F32))
    dtb = dt_bias.astype(F32)
    Dsk = D_skip.astype(F32)[:, None]

    def prep(v):
        b, l, _ = v.shape
        z, xbc, dt = jnp.split(v, [SSD_DIM, SSD_DIM + SSD_XBC], axis=-1)
        xbc = jax.nn.silu(dwconv1d(xbc, conv_w, conv_b)).astype(F32)
        xs, Bm, Cm = jnp.split(xbc, [SSD_DIM, SSD_DIM + SSD_GROUPS * SSD_STATE], axis=-1)
        xs = xs.reshape(b, l, SSD_HEADS, SSD_HEAD_DIM)
        Bm = Bm.reshape(b, l, SSD_GROUPS, SSD_STATE)
        Cm = Cm.reshape(b, l, SSD_GROUPS, SSD_STATE)
        dt = jax.nn.softplus(dt.astype(F32).reshape(b, l, 2, SSD_HEADS) + dtb)
        return z, xs, Bm, Cm, dt

    def run(v, h0f, h0b, need_y):
        z, xs, Bm, Cm, dt = prep(v)
        yf, sf = ssd_scan(xs, dt[:, :, 0], A[0], Bm, Cm, h0f, need_y)
        yb, sb = ssd_scan(_flip(xs), _flip(dt[:, :, 1]), A[1], _flip(Bm), _flip(Cm), h0b, need_y)
        if not need_y:
            return None, sf, sb
        y = yf + _flip(yb) + Dsk * xs
        y = y.reshape(z.shape).astype(z.dtype) * jax.nn.silu(z)
        return rms_norm(y, norm_g) @ w_out, sf, sb

    h0 = jnp.zeros((u.shape[0], SSD_HEADS, SSD_HEAD_DIM, SSD_STATE), F32)
    yc, sf, sb = run(uc, h0, h0, ctx_out)
    y, _, _ = run(u, sf, sb, True)
    return y, yc


def rwkv7_scan(r, w, k, v, kk, ka, S0, readout):
    def step(S, inp):
        r_t, w_t, k_t, v_t, kk_t, ka_t = inp
        sa = jnp.einsum('bhvk,bhk->bhv', S, -kk_t)
        S = S * w_t[:, :, None, :] + sa[..., None] * ka_t[:, :, None, :] + v_t[..., None] * k_t[:, :, None, :]
        y = jnp.einsum('bhvk,bhk->bhv', S, r_t) if readout else None
        return S, y
    seq = tuple(jnp.moveaxis(t, 1, 0) for t in (r, w, k, v, kk, ka))
    S, ys = lax.scan(step, S0, seq)
    return (jnp.moveaxis(ys, 0, 1) if readout else None), S


def rwkv_branch(u, uc, rows, ctx_out, mu, w0, w2, a0, a2, g2, k_k, k_a, r_k, ln_g, ln_b, w_out):
    def prep(v):
        b, l, _ = v.shape
        v = v + token_shift(v) * mu
        r, k, vv, wf, wb, al, gl = jnp.split(v, RWKV_OFFSETS, axis=-1)
        heads = lambda t: t.astype(F32).reshape(b, l, RWKV_HEADS, RWKV_HEAD_DIM)

        def decay(lo, d):
            wl = -jax.nn.softplus(-(w0[d] + jnp.tanh(lo) @ w2[d])) - 0.5
            return heads(jnp.exp(-jnp.exp(wl.astype(F32))))
        a = jax.nn.sigmoid(a0 + al @ a2)
        kk = heads(k * k_k)
        kk = kk * lax.rsqrt(jnp.sum(kk * kk, axis=-1, keepdims=True) + 1e-12)
        kmod = heads(k * (1 + (a - 1) * k_a))
        g = jax.nn.sigmoid(gl) @ g2
        return heads(r), decay(wf, 0), decay(wb, 1), kmod, heads(vv), kk, kk * heads(a), g

    def run(v, S0f, S0b, need_y):
        b, l, _ = v.shape
        r, wf, wb, k, vv, kk, ka, g = prep(v)
        yf, Sf = rwkv7_scan(r, wf, k, vv, kk, ka, S0f, need_y)
        yb, Sb = rwkv7_scan(*(_flip(t) for t in (r, wb, k, vv, kk, ka)), S0b, need_y)
        if not need_y:
            return None, Sf, Sb
        y = yf + _flip(yb)
        m = jnp.mean(y, axis=-1, keepdims=True)
        var = jnp.mean(jnp.square(y - m), axis=-1, keepdims=True)
        y = ((y - m) * lax.rsqrt(var + GN_EPS)).reshape(b, l, RWKV_DIM) * ln_g + ln_b
        bonus = (jnp.sum(r * k * r_k, axis=-1, keepdims=True) * vv).reshape(b, l, RWKV_DIM)
        return ((y + bonus) * g).astype(v.dtype), Sf, Sb

    S0 = jnp.zeros((u.shape[0], RWKV_HEADS, RWKV_HEAD_DIM, RWKV_HEAD_DIM), F32)
    oc, Sf, Sb = run(uc, S0, S0, ctx_out)
    o, _, _ = run(to_cols(u, rows), Sf, Sb, True)
    y = to_rows(o, rows) @ w_out
    yc = oc @ w_out if ctx_out else None
    return y, yc


def conformer_branch(u, seg, conv_w, conv_b, ln_g, ln_b, w_out):
    b, l, _ = u.shape
    val, gate = jnp.split(u, 2, axis=-1)
    v = (val * jax.nn.sigmoid(gate)).reshape(-1, seg, CONV_DIM)
    v = dwconv1d(v, conv_w, conv_b).reshape(b, l, CONV_DIM)
    v = layer_norm(v, ln_g, ln_b, LN_EPS)
    return jax.nn.silu(v) @ w_out


def fourier_branch(u, w_out):
    b, l, _ = u.shape
    uf = u.astype(F32).reshape(b, l, FOURIER_GROUPS, FOURIER_GROUP_DIM)
    y = jnp.fft.fftn(uf, axes=(1, 3), norm='ortho').real
    return y.reshape(b, l, FOURIER_DIM).astype(u.dtype) @ w_out


def gated_merge(gate_pre, branches, w_o):
    b, l, _ = gate_pre.shape
    gates = jax.nn.sigmoid(gate_pre).reshape(b, l, N_BRANCH, D_MODEL)
    m = gates[:, :, 0] * branches[0]
    for i in range(1, N_BRANCH):
        m = m + gates[:, :, i] * branches[i]
    return m @ w_o


def hybrid_mixer(h, hc, rows, ctx_out, w_in, conv_p, ssd_p, fourier_out, rwkv_p, w_o):
    u = h @ w_in
    uc = hc @ (w_in if ctx_out else w_in[:, :REC_IN])
    u_ssd, u_rwkv, u_conv, u_fft, u_gate = jnp.split(u, IN_OFFSETS, axis=-1)
    uc_ssd, uc_rwkv = uc[..., :SSD_IN], uc[..., SSD_IN:REC_IN]
    p_ssd, pc_ssd = ssd_branch(u_ssd, uc_ssd, ctx_out, *ssd_p)
    p_rwkv, pc_rwkv = rwkv_branch(u_rwkv, uc_rwkv, rows, ctx_out, *rwkv_p)
    p_conv = conformer_branch(u_conv, GRID_W, *conv_p)
    p_fft = fourier_branch(u_fft, fourier_out)
    y = gated_merge(u_gate, (p_conv, p_ssd, p_fft, p_rwkv), w_o)
    if not ctx_out:
        return y, None
    uc_conv, uc_fft, uc_gate = uc[..., IN_OFFSETS[1]:IN_OFFSETS[2]], uc[..., IN_OFFSETS[2]:IN_OFFSETS[3]], uc[..., IN_OFFSETS[3]:]
    pc_conv = conformer_branch(uc_conv, uc_conv.shape[1], *conv_p)
    pc_fft = fourier_branch(uc_fft, fourier_out)
    yc = gated_merge(uc_gate, (pc_conv, pc_ssd, pc_fft, pc_rwkv), w_o)
    return y, yc


def sq_relu_mlp(h, w_up, w_down):
    return jnp.square(jax.nn.relu(h @ w_up)) @ w_down


def setup_inputs(seed: int = 0) -> dict:
    key = jax.random.key(seed)
    ks = iter(jax.random.split(key, 48))
    nrm = lambda shape, scale: jax.random.normal(next(ks), shape, F32) * scale
    L, D = DEPTH, D_MODEL
    x = nrm((BATCH, SEQ, D), 1.0)
    c = nrm((BATCH, D), 1.0)
    ctx = nrm((BATCH, CTX_LEN, D), 1.0)
    c_ctx = nrm((D,), 1.0)
    mod_w = nrm((L, D, 6 * D), 0.5 * D ** -0.5)
    mod_b = nrm((L, 6 * D), 0.02)
    norm_g = 1.0 + nrm((L, 4, D), 0.02)
    w_in = nrm((L, D, IN_DIM), D ** -0.5)
    conv_w = nrm((L, CONV_WIDTH, CONV_DIM), CONV_WIDTH ** -0.5)
    conv_b = nrm((L, CONV_DIM), 0.02)
    conv_ln_g = 1.0 + nrm((L, CONV_DIM), 0.02)
    conv_ln_b = nrm((L, CONV_DIM), 0.02)
    conv_out = nrm((L, CONV_DIM, D), CONV_DIM ** -0.5)
    ssd_conv_w = nrm((L, SSD_CONV, SSD_XBC), SSD_CONV ** -0.5)
    ssd_conv_b = nrm((L, SSD_XBC), 0.02)
    ssd_A_log = jnp.log(jax.random.uniform(next(ks), (L, 2, SSD_HEADS), F32, 1.0, 16.0))
    dt0 = jnp.exp(jax.random.uniform(next(ks), (L, 2, SSD_HEADS), F32, float(np.log(1e-3)), float(np.log(1e-1))))
    ssd_dt_bias = dt0 + jnp.log(-jnp.expm1(-dt0))
    ssd_D = 1.0 + nrm((L, SSD_HEADS), 0.1)
    ssd_norm_g = 1.0 + nrm((L, SSD_DIM), 0.02)
    ssd_out = nrm((L, SSD_DIM, D), SSD_DIM ** -0.5)
    fourier_out = nrm((L, FOURIER_DIM, D), FOURIER_DIM ** -0.5)
    rwkv_mu = jax.random.uniform(next(ks), (L, RWKV_IN), F32, 0.0, 1.0)
    rwkv_w0 = nrm((L, 2, RWKV_DIM), 0.5)
    rwkv_w2 = nrm((L, 2, DECAY_LORA, RWKV_DIM), 0.1 * DECAY_LORA ** -0.5)
    rwkv_a0 = nrm((L, RWKV_DIM), 0.1)
    rwkv_a2 = nrm((L, ICL_LORA, RWKV_DIM), 0.1 * ICL_LORA ** -0.5)
    rwkv_g2 = nrm((L, GATE_LORA, RWKV_DIM), GATE_LORA ** -0.5)
    rwkv_k_k = 0.85 + nrm((L, RWKV_DIM), 0.02)
    rwkv_k_a = 1.0 + nrm((L, RWKV_DIM), 0.02)
    rwkv_r_k = nrm((L, RWKV_HEADS, RWKV_HEAD_DIM), 0.1)
    rwkv_ln_g = 1.0 + nrm((L, RWKV_DIM), 0.02)
    rwkv_ln_b = nrm((L, RWKV_DIM), 0.02)
    rwkv_out = nrm((L, RWKV_DIM, D), RWKV_DIM ** -0.5)
    w_o = nrm((L, D, D), D ** -0.5)
    mlp_up = nrm((L, D, D_FF), D ** -0.5)
    mlp_down = nrm((L, D_FF, D), D_FF ** -0.5)
    return {'x': x, 'c': c, 'ctx': ctx, 'c_ctx': c_ctx, 'mod_w': mod_w, 'mod_b': mod_b,
            'norm_g': norm_g, 'w_in': w_in,
            'conv_w': conv_w, 'conv_b': conv_b, 'conv_ln_g': conv_ln_g, 'conv_ln_b': conv_ln_b, 'conv_out': conv_out,
            'ssd_conv_w': ssd_conv_w, 'ssd_conv_b': ssd_conv_b, 'ssd_A_log': ssd_A_log, 'ssd_dt_bias': ssd_dt_bias,
            'ssd_D': ssd_D, 'ssd_norm_g': ssd_norm_g, 'ssd_out': ssd_out,
            'fourier_out': fourier_out,
            'rwkv_mu': rwkv_mu, 'rwkv_w0': rwkv_w0, 'rwkv_w2': rwkv_w2, 'rwkv_a0': rwkv_a0, 'rwkv_a2': rwkv_a2,
            'rwkv_g2': rwkv_g2, 'rwkv_k_k': rwkv_k_k, 'rwkv_k_a': rwkv_k_a, 'rwkv_r_k': rwkv_r_k,
            'rwkv_ln_g': rwkv_ln_g, 'rwkv_ln_b': rwkv_ln_b, 'rwkv_out': rwkv_out,
            'w_o': w_o, 'mlp_up': mlp_up, 'mlp_down': mlp_down}


def reference(x, c, ctx, c_ctx, mod_w, mod_b, norm_g, w_in,
              conv_w, conv_b, conv_ln_g, conv_ln_b, conv_out,
              ssd_conv_w, ssd_conv_b, ssd_A_log, ssd_dt_bias, ssd_D, ssd_norm_g, ssd_out,
              fourier_out,
              rwkv_mu, rwkv_w0, rwkv_w2, rwkv_a0, rwkv_a2, rwkv_g2, rwkv_k_k, rwkv_k_a, rwkv_r_k,
              rwkv_ln_g, rwkv_ln_b, rwkv_out,
              w_o, mlp_up, mlp_down):
    rows = x.shape[1] // GRID_W
    xc = ctx
    silu_c = jax.nn.silu(c)
    silu_cc = jax.nn.silu(c_ctx)
    for i in range(DEPTH):
        last = i == DEPTH - 1
        mod = silu_c @ mod_w[i] + mod_b[i]
        modc = silu_cc @ mod_w[i] + mod_b[i]
        sh1, sc1, g1, sh2, sc2, g2 = jnp.split(mod[:, None, :], 6, axis=-1)
        csh1, csc1, cg1, csh2, csc2, cg2 = jnp.split(modc, 6, axis=-1)
        conv_p = (conv_w[i], conv_b[i], conv_ln_g[i], conv_ln_b[i], conv_out[i])
        ssd_p = (ssd_conv_w[i], ssd_conv_b[i], ssd_A_log[i], ssd_dt_bias[i], ssd_D[i], ssd_norm_g[i], ssd_out[i])
        rwkv_p = (rwkv_mu[i], rwkv_w0[i], rwkv_w2[i], rwkv_a0[i], rwkv_a2[i], rwkv_g2[i], rwkv_k_k[i],
                  rwkv_k_a[i], rwkv_r_k[i], rwkv_ln_g[i], rwkv_ln_b[i], rwkv_out[i])
        h = rms_norm(x, norm_g[i, 0]) * (1 + sc1) + sh1
        hc = rms_norm(xc, norm_g[i, 0]) * (1 + csc1) + csh1
        y, yc = hybrid_mixer(h, hc, rows, not last, w_in[i], conv_p, ssd_p, fourier_out[i], rwkv_p, w_o[i])
        x = x + g1 * rms_norm(y, norm_g[i, 1])
        h = rms_norm(x, norm_g[i, 2]) * (1 + sc2) + sh2
        x = x + g2 * rms_norm(sq_relu_mlp(h, mlp_up[i], mlp_down[i]), norm_g[i, 3])
        if not last:
            xc = xc + cg1 * rms_norm(yc, norm_g[i, 1])
            hc = rms_norm(xc, norm_g[i, 2]) * (1 + csc2) + csh2
            xc = xc + cg2 * rms_norm(sq_relu_mlp(hc, mlp_up[i], mlp_down[i]), norm_g[i, 3])
    return x
```

```python
import contextlib
import numpy as np
import concourse.bass as bass
import concourse.mybir as mybir
from concourse.bass_utils import run_bass_kernel_spmd

F32 = mybir.dt.float32
BF16 = mybir.dt.bfloat16
AF = mybir.ActivationFunctionType
ALU = mybir.AluOpType
AX = mybir.AxisListType

ENGS = ("pe", "act", "dve", "pool", "sp")


class _Rec:
    def __getattr__(self, name):
        return lambda *a, **k: (name, a, k)


_REC = _Rec()


class Prog:
    NDMA = {"sp": 20, "pool": 12, "act": 6}

    def __init__(self, nc, stack):
        self.nc = nc
        self.stack = stack
        self.q = {e: [] for e in ENGS}
        self.cnt = {e: 0 for e in ENGS}
        self.esem = {e: stack.enter_context(nc.semaphore("es_" + e)) for e in ENGS}
        self.dsem = {e: [stack.enter_context(nc.semaphore("ds_%s%d" % (e, i))) for i in range(n)]
                     for e, n in self.NDMA.items()}
        self.dcnt = {e: 0 for e in self.NDMA}
        self.seen = {e: {} for e in ENGS}
        self.last_w = {}
        self.rd_eng = {}
        self.rd_dma = {}
        self.n_wait = 0

    def _deps(self, e, reads, writes, is_dma):
        need = []
        for k in reads:
            for t in self.last_w.get(k, ()):
                need.append(t)
        for k in writes:
            for t in self.last_w.get(k, ()):
                if (t[3] != e or is_dma or t[3] is None):
                    need.append(t)
            for re_, v in self.rd_eng.get(k, {}).items():
                if re_ != e or is_dma:
                    need.append(("es_" + re_, self.esem[re_], v, re_))
            need.extend(self.rd_dma.get(k, ()))
        waits = []
        seen = self.seen[e]
        for (nm, sem, v, _) in need:
            if seen.get(nm, 0) < v:
                seen[nm] = v
                waits.append((sem, v))
        return waits

    def _commit(self, tok, reads, writes, is_dma):
        for k in writes:
            self.last_w[k] = [tok]
            self.rd_eng[k] = {}
            self.rd_dma[k] = []
        for k in reads:
            if is_dma:
                l = self.rd_dma.setdefault(k, [])
                l.append(tok)
                if len(l) > 48:
                    del l[0]
            else:
                self.rd_eng.setdefault(k, {})[tok[3]] = tok[2]

    def op(self, e, fn, reads=(), writes=()):
        waits = self._deps(e, reads, writes, False)
        self.cnt[e] += 1
        tok = ("es_" + e, self.esem[e], self.cnt[e], e)
        self.q[e].append((waits, fn(_REC), self.esem[e], 1))
        self.n_wait += len(waits)
        self._commit(tok, reads, writes, False)

    def dma(self, e, out, in_, reads=(), writes=(), **kw):
        waits = self._deps(e, reads, writes, True)
        i = self.dcnt[e]
        self.dcnt[e] += 1
        K = len(self.dsem[e])
        sem = self.dsem[e][i % K]
        nm = "ds_%s%d" % (e, i % K)
        if i >= K:
            v0 = 16 * (i // K)
            if self.seen[e].get(nm, 0) < v0:
                self.seen[e][nm] = v0
                waits.append((sem, v0))
        tok = (nm, sem, 16 * (i // K + 1), None)
        self.q[e].append((waits, ("dma_start", (), dict(out=out, in_=in_, **kw)), sem, 16))
        self._commit(tok, reads, writes, True)

    def cc(self, kind, ins, outs, groups, reads=(), writes=()):
        e = "pool"
        waits = self._deps(e, reads, writes, True)
        if not hasattr(self, "csem"):
            self.csem = [self.stack.enter_context(self.nc.semaphore("cs_%d" % i)) for i in range(4)]
            self.ccnt = 0
        i = self.ccnt
        self.ccnt += 1
        K = len(self.csem)
        sem = self.csem[i % K]
        nm = "cs_%d" % (i % K)
        if i >= K:
            v0 = (i // K)
            if self.seen[e].get(nm, 0) < v0:
                self.seen[e][nm] = v0
                waits.append((sem, v0))
        tok = (nm, sem, (i // K + 1), None)
        self.q[e].append((waits, ("collective_compute", (kind, ALU.bypass), dict(replica_groups=groups, ins=ins, outs=outs)), sem, 1))
        self._commit(tok, reads, writes, True)

    def barrier(self):
        toks = []
        for e in ENGS:
            if self.cnt[e] > 0:
                toks.append(("es_" + e, self.esem[e], self.cnt[e]))
        for e, sems in self.dsem.items():
            n, K = self.dcnt[e], len(sems)
            for j, sem in enumerate(sems):
                uses = (n - j + K - 1) // K if n > j else 0
                if uses > 0:
                    toks.append(("ds_%s%d" % (e, j), sem, 16 * uses))
        if hasattr(self, "csem"):
            n, K = self.ccnt, len(self.csem)
            for j, sem in enumerate(self.csem):
                uses = (n - j + K - 1) // K if n > j else 0
                if uses > 0:
                    toks.append(("cs_%d" % j, sem, uses))
        for e in ENGS:
            waits = []
            for nm, sem, v in toks:
                if nm == "es_" + e:
                    continue
                if self.seen[e].get(nm, 0) < v:
                    self.seen[e][nm] = v
                    waits.append((sem, v))
            self.q[e].append((waits, None, None, 0))
        self.last_w.clear()
        self.rd_eng.clear()
        self.rd_dma.clear()

    def wait_all(self, e, keys):
        waits = self._deps(e, keys, (), True)
        self.q[e].append((waits, None, None, 0))

    def emit(self):
        nc = self.nc
        with nc.Block() as block:
            def run(eng, name):
                for waits, fn, sem, inc in self.q[name]:
                    for (s, v) in waits:
                        eng.wait_ge(s, v)
                    if fn is not None:
                        getattr(eng, fn[0])(*fn[1], **fn[2]).then_inc(sem, inc)

            @block.tensor
            def _(eng):
                run(eng, "pe")

            @block.scalar
            def _(eng):
                run(eng, "act")

            @block.vector
            def _(eng):
                run(eng, "dve")

            @block.gpsimd
            def _(eng):
                run(eng, "pool")

            @block.sync
            def _(eng):
                run(eng, "sp")
        for e in ENGS:
            self.q[e] = []


D = 2048
KC = 16
T = 2304
NCTX = 256
NLAT = 2048
IN_DIM = 14168
FFT0 = 5464
GATE0 = 5976
RW0 = 2584
CV0 = 4440
TBS = [(0, 256), (256, 512), (768, 512), (1280, 512), (1792, 512)]
RMS_EPS = 1e-6
NCORE = 8
YC0, YS0, YF0, YR0 = 0, 512, 1280, 1792
YROWS = 2304
BR = [("conv_out", YC0, 512), ("ssd_out", YS0, 768), ("fourier_out", YF0, 512), ("rwkv_out", YR0, 512)]


def build(cfg):
    LS = cfg["layers"]
    NL = len(LS)
    mix = cfg.get("mix", ("conv", "fft", "ssd", "rwkv"))
    phases = cfg.get("phases", ("p1", "mix", "p3", "p4"))
    tbs_sel = cfg.get("tbs", range(5))
    nc = bass.Bass("TRN2", target_bir_lowering=False)
    stack = contextlib.ExitStack()
    with stack:
        P = Prog(nc, stack)

        def din(name, shape, dt=F32):
            return nc.dram_tensor(name, list(shape), dt, kind="ExternalInput").ap()

        def dout(name, shape, dt=F32):
            return nc.dram_tensor(name, list(shape), dt, kind="ExternalOutput").ap()

        def dscr(name, shape, dt=F32):
            return nc.dram_tensor(name, list(shape), dt).ap()

        uniq = [0]

        def sbp(st, name, shape, dt=F32):
            uniq[0] += 1
            return st.enter_context(nc.sbuf_tensor("sb%d_%s" % (uniq[0], name), list(shape), dt))

        def sb(name, shape, dt=F32):
            return sbp(stack, name, shape, dt)

        groups = [list(range(NCORE))]
        need_w = ("p1" in phases) or ("p3" in phases)

        x_in = din("x_in", [KC, 128, T])
        cT_in = din("cT", [128, KC, 5])
        sel_in = din("sel", [128, 5])
        ng_in = din("ng", [128, NL, 4, KC])
        modw_in = din("modw", [NL, D, 1536])
        modb_in = din("modb", [128, NL, 12])
        y_out = dout("y_out", [KC, 128, NLAT])
        WSPEC = {}
        if need_w:
            WSPEC["w_in"] = (256, IN_DIM)
        if "p3" in phases:
            WSPEC.update({"w_o": (256, D), "conv_out": (64, D), "ssd_out": (96, D), "fourier_out": (64, D), "rwkv_out": (64, D)})
        if "p4" in phases:
            WSPEC.update({"mlp_up": (256, 4 * D), "mlp_down": (1024, D)})
        w_sh, w_c, w_g = {}, {}, {}
        for nm, (r, c) in WSPEC.items():
            w_sh[nm] = din(nm, [NL, r, c])
            w_c[nm] = [dscr("%s_c%d" % (nm, l), [r, c], BF16) for l in range(NL)]
            w_g[nm] = [dscr("%s_g%d" % (nm, l), [8 * r, c], BF16) for l in range(NL)]
        if "fft" in mix:
            dft_sh = {nm: din(nm, [256, 2048]) for nm in ("dftC", "dftS")}
            dft_c = {nm: dscr(nm + "_c", [256, 2048], BF16) for nm in dft_sh}
            dft_g = {nm: dscr(nm + "_g", [2048, 2048], BF16) for nm in dft_sh}
            dft_small_in = din("dft_small", [128, 2 * 128 + 2 * 2 * 256])
        if "conv" in mix:
            convp_in = din("convp", [128, NL, 4, 34])
        if "ssd" in mix or "rwkv" in mix:
            cst128_in = din("cst128", [128, 3, 128])
        if "rwkv" in mix:
            rwp_in = din("rwp", [128, NL, 4, 9])
            rwmu_in = din("rwmu", [128, NL, 15])
            rww_in = din("rww", [128, NL, 3, 512])
            blk_in = din("blkones", [128, 128])
            RWo = dscr("RWo", [2, 4, 512, T])
            RWv = dscr("RWv", [3, 512, T])
            RYtmp = dscr("RYtmp", [512, T])
        if "ssd" in mix:
            ssdp_in = din("ssdp", [128, NL, 14, 6])
            ssdv_in = din("ssdv", [128, NL, 60])
            Ytmp = dscr("Ytmp", [768, T])

        xres = dscr("xres", [KC, 128, T])
        if cfg.get("U_in"):
            U = din("U", [FFT0, T])
            Xtm = din("Xtm", [T, 512], BF16)
            DTtm = din("DTtm", [T, 24])
        else:
            U = dscr("U", [FFT0, T])
            Xtm = dscr("Xtm", [T, 512], BF16)
            DTtm = dscr("DTtm", [T, 24])
        Y = dout("Y", [YROWS, T]) if cfg.get("Y_out") else dscr("Y", [YROWS, T])
        mod_loc = dscr("mod_loc", [128, NL * 12 * 5])
        mod_all = dscr("mod_all", [NCORE * 128, NL * 12 * 5])

        ones_f = sb("ones_f", [128, 128])
        P.op("dve", lambda e: e.memset(ones_f[:], 1.0), writes=["ones_f"])
        ps = [stack.enter_context(nc.psum_tensor("pp_ps%d" % i, [128, 512], F32)) for i in range(8)]
        psk = [("ps", i) for i in range(8)]
        sel = sb("sel", [128, 5])
        ngs = sb("ngs", [128, NL, 4, KC])
        modT = sb("modT", [128, NCORE, NL * 12, 5])
        modo = sb("modo", [128, NCORE, NL * 12])
        der = sb("der", [128, NL, 2, 4, KC])
        MT = [("modT", r) for r in range(NCORE)]
        cnt = {"w": 0, "ps": 0, "stg": 0, "sq": 0, "c": 0}

        def nps():
            i = cnt["ps"] % cnt.get("psmod", 8)
            cnt["ps"] += 1
            return i

        def modv(l, which, kc, ctx):
            j = which * 16 + kc
            r, jj = j // 12, j % 12
            if ctx:
                return modT[:, r, l * 12 + jj, 4:5]
            return modo[:, r, l * 12 + jj:l * 12 + jj + 1]

        with contextlib.ExitStack() as ph:
            cst_f = [sbp(ph, "cst_f%d" % i, [128, 2048]) for i in range(2)]
            cst_b = [sbp(ph, "cst_b%d" % i, [128, 2048], BF16) for i in range(2)]

            def prep_weight(src, dst_c, dst_g, key):
                rows, cols = src.shape
                tiles = []
                for r0 in range(0, rows, 128):
                    rn = min(128, rows - r0)
                    for c0 in range(0, cols, 2048):
                        cn = min(2048, cols - c0)
                        i = cnt["c"] % 2
                        cnt["c"] += 1
                        P.dma("sp", cst_f[i][:rn, :cn], src[r0:r0 + rn, c0:c0 + cn], writes=[("cst_f", i)])
                        if i == 0:
                            P.op("dve", lambda e: e.tensor_copy(out=cst_b[i][:rn, :cn], in_=cst_f[i][:rn, :cn]),
                                 reads=[("cst_f", i)], writes=[("cst_b", i)])
                        else:
                            P.op("act", lambda e: e.copy(out=cst_b[i][:rn, :cn], in_=cst_f[i][:rn, :cn]),
                                 reads=[("cst_f", i)], writes=[("cst_b", i)])
                        k2 = (key, "c", r0, c0)
                        P.dma("sp", dst_c[r0:r0 + rn, c0:c0 + cn], cst_b[i][:rn, :cn], reads=[("cst_b", i)], writes=[k2])
                        tiles.append(k2)
                P.cc("AllGather", [dst_c], [dst_g], groups, reads=tiles, writes=[key])

            for l in range(NL):
                for nm in WSPEC:
                    prep_weight(w_sh[nm][l], w_c[nm][l], w_g[nm][l], (nm, l))
            if "fft" in mix:
                for nm in dft_sh:
                    prep_weight(dft_sh[nm], dft_c[nm], dft_g[nm], nm)
            P.dma("sp", xres, x_in, writes=["xres"])

            sT = sbp(ph, "sT", [128, KC, 5])
            P.dma("sp", sT[:], cT_in, writes=["sT"])
            P.op("act", lambda e: e.activation(out=sT[:], in_=sT[:], func=AF.Silu), reads=["sT"], writes=["sT"])
            P.dma("sp", sel[:], sel_in, writes=["sel"])
            P.dma("sp", ngs[:], ng_in, writes=["ngs"])
            modb = sbp(ph, "modb", [128, NL, 12])
            P.dma("sp", modb[:], modb_in, writes=["modb"])
            modloc = sbp(ph, "modloc", [128, NL, 12, 5])
            mwt = [sbp(ph, "mwt%d" % i, [128, 1536]) for i in range(2)]
            modacc = sbp(ph, "modacc", [128, 96])
            v5 = lambda ap: ap.rearrange("p (j e) -> p j e", e=8)[:, :, 0:5]
            for l in range(NL):
                for kc in range(KC):
                    i = (l * KC + kc) % 2
                    P.dma("sp", mwt[i][:], modw_in[l, kc * 128:(kc + 1) * 128, :], writes=[("mwt", i)])
                    pi = nps()
                    for jj in range(12):
                        P.op("pe", lambda e: e.matmul(ps[pi][:, jj * 8:jj * 8 + 5], mwt[i][:, jj * 128:(jj + 1) * 128],
                                                      sT[:, kc, :], start=True, stop=True),
                             reads=[("mwt", i), "sT"], writes=[psk[pi]])
                    if kc == 0:
                        P.op("dve", lambda e: e.tensor_scalar(out=v5(modacc[:]), in0=v5(ps[pi][:, 0:96]), scalar1=1.0,
                                                              scalar2=None, op0=ALU.mult),
                             reads=[psk[pi]], writes=["modacc"])
                    else:
                        P.op("dve", lambda e: e.tensor_tensor(out=v5(modacc[:]), in0=v5(modacc[:]), in1=v5(ps[pi][:, 0:96]),
                                                              op=ALU.add), reads=[psk[pi], "modacc"], writes=["modacc"])
                for jj in range(12):
                    P.op("dve", lambda e: e.tensor_scalar(out=modloc[:, l, jj, :], in0=modacc[:, jj * 8:jj * 8 + 5],
                                                          scalar1=modb[:, l, jj:jj + 1], scalar2=None, op0=ALU.add),
                         reads=["modacc", "modb"], writes=["modloc"])
            P.dma("sp", mod_loc, modloc[:].rearrange("p l j f -> p (l j f)"), reads=["modloc"], writes=["mod_loc"])
            P.cc("AllGather", [mod_loc], [mod_all], groups, reads=["mod_loc"], writes=["mod_all"])
            for r in range(NCORE):
                P.dma("sp", modT[:, r, :, :].rearrange("p a f -> p (a f)"), mod_all[r * 128:(r + 1) * 128, :],
                      reads=["mod_all"], writes=[("modT", r)])
            for q in range(4):
                if q == 0:
                    P.op("dve", lambda e: e.tensor_scalar(out=modo[:], in0=modT[:, :, :, 0], scalar1=sel[:, 0:1],
                                                          scalar2=None, op0=ALU.mult), reads=MT + ["sel"], writes=["modo"])
                else:
                    P.op("dve", lambda e: e.scalar_tensor_tensor(out=modo[:], in0=modT[:, :, :, q], scalar=sel[:, q:q + 1],
                                                                 in1=modo[:], op0=ALU.mult, op1=ALU.add),
                         reads=MT + ["sel", "modo"], writes=["modo"])
            for l in range(NL):
                for cx in range(2):
                    for kc in range(KC):
                        for (slot, which, gi, plus1) in ((0, 1, 0, True), (1, 2, 1, False), (2, 4, 2, True), (3, 5, 3, False)):
                            if plus1:
                                P.op("dve", lambda e: e.tensor_scalar(
                                    out=der[:, l, cx, slot, kc:kc + 1], in0=modv(l, which, kc, cx), scalar1=1.0,
                                    scalar2=ngs[:, l, gi, kc:kc + 1], op0=ALU.add, op1=ALU.mult),
                                    reads=MT + ["modo", "ngs"], writes=["der"])
                            else:
                                P.op("dve", lambda e: e.tensor_scalar(
                                    out=der[:, l, cx, slot, kc:kc + 1], in0=modv(l, which, kc, cx),
                                    scalar1=ngs[:, l, gi, kc:kc + 1], scalar2=None, op0=ALU.mult),
                                    reads=MT + ["modo", "ngs"], writes=["der"])
            P.barrier()
            P.emit()

        class Dense:
            def __init__(self, ph, with_hid=False, with_m=False):
                self.xt = sbp(ph, "xt", [128, KC, 512])
                self.sq = [sbp(ph, "sq%d" % i, [128, 512]) for i in range(2)]
                self.rstd = sbp(ph, "rstd", [128, 512])
                self.hT = sbp(ph, "hT", [128, KC, 512], BF16)
                self.wblk = [sbp(ph, "wblk%d" % i, [128, KC, 512], BF16) for i in range(3)]
                self.stg = [sbp(ph, "stg%d" % i, [128, 512]) for i in range(4)]
                if with_hid:
                    self.hid = sbp(ph, "hid", [128, 4 * KC, 512], BF16)
                if with_m:
                    self.mT = sbp(ph, "mT", [128, KC, 512], BF16)
                    self.ybr = sbp(ph, "ybr", [128, 18, 512], BF16)
                    self.wb2 = [sbp(ph, "wb2%d" % i, [128, 6, 512], BF16) for i in range(2)]
                    self.sig = [sbp(ph, "sig%d" % i, [128, 512]) for i in range(2)]
                    self.stgb = None
                else:
                    self.stgb = [sbp(ph, "stgb%d" % i, [128, 512], BF16) for i in range(2)]

            def nstg(self):
                i = cnt["stg"] % 4
                cnt["stg"] += 1
                return i

            def load_wblk(self, src_ap, key):
                i = cnt["w"] % 3
                cnt["w"] += 1
                kk = src_ap.shape[0] // 128
                w = src_ap.shape[1]
                P.dma("sp", self.wblk[i][:, :kk, :w], src_ap.rearrange("(k p) c -> p k c", p=128),
                      reads=[key], writes=[("wblk", i)])
                return i

            def rstd_from(self, src, n, srckey):
                pi = nps()
                for kc in range(KC):
                    si = cnt["sq"] % 2
                    cnt["sq"] += 1
                    P.op("act", lambda e: e.activation(out=self.sq[si][:, :n], in_=src[:, kc, :n], func=AF.Square),
                         reads=[srckey], writes=[("sq", si)])
                    P.op("pe", lambda e: e.matmul(ps[pi][:, :n], ones_f[:], self.sq[si][:, :n],
                                                  start=(kc == 0), stop=(kc == KC - 1)),
                         reads=["ones_f", ("sq", si)], writes=[psk[pi]])
                rstd = self.rstd
                P.op("dve", lambda e: e.tensor_scalar(out=rstd[:, :n], in0=ps[pi][:, :n], scalar1=1.0 / D, scalar2=RMS_EPS,
                                                      op0=ALU.mult, op1=ALU.add), reads=[psk[pi]], writes=["rstd"])
                P.op("act", lambda e: e.activation(out=rstd[:, :n], in_=rstd[:, :n], func=AF.Sqrt), reads=["rstd"], writes=["rstd"])
                P.op("dve", lambda e: e.reciprocal(out=rstd[:, :n], in_=rstd[:, :n]), reads=["rstd"], writes=["rstd"])

            def norm_mod(self, l, tbi, slotA, whichB):
                t0, n = TBS[tbi]
                cx = 1 if tbi == 0 else 0
                xt, hT, rstd = self.xt, self.hT, self.rstd
                P.dma("sp", xt[:, :, :n], xres[:, :, t0:t0 + n].rearrange("k p t -> p k t"),
                      reads=["xres"] + [("xres", kc, tbi) for kc in range(KC)], writes=["xt"])
                self.rstd_from(xt, n, "xt")
                for kc in range(KC):
                    P.op("dve", lambda e: e.tensor_tensor(out=xt[:, kc, :n], in0=xt[:, kc, :n], in1=rstd[:, :n], op=ALU.mult),
                         reads=["xt", "rstd"], writes=["xt"])
                    P.op("act", lambda e: e.activation(out=hT[:, kc, :n], in_=xt[:, kc, :n], func=AF.Identity,
                                                       bias=modv(l, whichB, kc, cx), scale=der[:, l, cx, slotA, kc:kc + 1]),
                         reads=["xt", "der", "modo"] + MT, writes=["hT"])

            def norm_residual(self, l, tbi, slotG):
                t0, n = TBS[tbi]
                cx = 1 if tbi == 0 else 0
                yb, rstd, stg = self.xt, self.rstd, self.stg
                self.rstd_from(yb, n, "xt")
                for kc in range(KC):
                    si = self.nstg()
                    P.dma("sp", stg[si][:, :n], xres[kc, :, t0:t0 + n], reads=["xres", ("xres", kc, tbi)], writes=[("stg", si)])
                    P.op("dve", lambda e: e.tensor_tensor(out=yb[:, kc, :n], in0=yb[:, kc, :n], in1=rstd[:, :n], op=ALU.mult),
                         reads=["xt", "rstd"], writes=["xt"])
                    P.op("dve", lambda e: e.scalar_tensor_tensor(out=stg[si][:, :n], in0=yb[:, kc, :n],
                                                                 scalar=der[:, l, cx, slotG, kc:kc + 1], in1=stg[si][:, :n],
                                                                 op0=ALU.mult, op1=ALU.add),
                         reads=["xt", ("stg", si), "der"], writes=[("stg", si)])
                    P.dma("sp", xres[kc, :, t0:t0 + n], stg[si][:, :n], reads=[("stg", si)], writes=[("xres", kc, tbi)])

            def evac(self, dst_ap, pi, m, n, dstkey, alt):
                if alt % 2 == 0:
                    P.op("act", lambda e: e.activation(out=dst_ap, in_=ps[pi][:m, :n], func=AF.Identity),
                         reads=[psk[pi]], writes=[dstkey])
                else:
                    P.op("dve", lambda e: e.tensor_scalar(out=dst_ap, in0=ps[pi][:m, :n], scalar1=1.0, scalar2=None,
                                                          op0=ALU.mult), reads=[psk[pi]], writes=[dstkey])

        def phase_p1(l):
            with contextlib.ExitStack() as ph:
                dn = Dense(ph)
                wg = w_g["w_in"][l]
                for tbi in tbs_sel:
                    t0, n = TBS[tbi]
                    dn.norm_mod(l, tbi, 0, 0)
                    hT = dn.hT
                    for c0 in list(range(0, FFT0, 512)) + [FFT0]:
                        w = 512 if c0 == FFT0 else min(512, FFT0 - c0)
                        wi = dn.load_wblk(wg[:, c0:c0 + w], ("w_in", l))
                        wb = dn.wblk[wi]
                        if c0 == FFT0 or c0 == 2560:
                            wtm = 512 if c0 == FFT0 else 24
                            for tt in range(n // 128):
                                pi = nps()
                                for kc in range(KC):
                                    P.op("pe", lambda e: e.matmul(ps[pi][:, :wtm], hT[:, kc, tt * 128:(tt + 1) * 128],
                                                                  wb[:, kc, :wtm], start=(kc == 0), stop=(kc == KC - 1)),
                                         reads=["hT", ("wblk", wi)], writes=[psk[pi]])
                                r0 = t0 + tt * 128
                                if c0 == FFT0:
                                    si = cnt["stg"] % 2
                                    cnt["stg"] += 1
                                    P.op("act", lambda e: e.activation(out=dn.stgb[si][:, :512], in_=ps[pi][:, :512], func=AF.Identity),
                                         reads=[psk[pi]], writes=[("stgb", si)])
                                    P.dma("sp", Xtm[r0:r0 + 128, :], dn.stgb[si][:, :512], reads=[("stgb", si)], writes=[("Xtm", r0)])
                                else:
                                    si = dn.nstg()
                                    dn.evac(dn.stg[si][:, :24], pi, 128, 24, ("stg", si), 1)
                                    P.dma("sp", DTtm[r0:r0 + 128, :], dn.stg[si][:, :24], reads=[("stg", si)], writes=[("DTtm", r0)])
                            if c0 == FFT0:
                                continue
                        for m0 in range(0, w, 128):
                            m = min(128, w - m0)
                            pi = nps()
                            for kc in range(KC):
                                P.op("pe", lambda e: e.matmul(ps[pi][:m, :n], wb[:, kc, m0:m0 + m], hT[:, kc, :n],
                                                              start=(kc == 0), stop=(kc == KC - 1)),
                                     reads=["hT", ("wblk", wi)], writes=[psk[pi]])
                            si = dn.nstg()
                            dn.evac(dn.stg[si][:m, :n], pi, m, n, ("stg", si), si)
                            P.dma("sp", U[c0 + m0:c0 + m0 + m, t0:t0 + n], dn.stg[si][:m, :n],
                                  reads=[("stg", si)], writes=[("U", c0 + m0, tbi)])
                P.barrier()
                P.emit()

        def phase_p4(l):
            last = (LS[l] in cfg.get('last_ids', (3,)))
            with contextlib.ExitStack() as ph:
                dn = Dense(ph, with_hid=True)
                hT, hid = dn.hT, dn.hid
                for tbi in tbs_sel:
                    if last and tbi == 0:
                        continue
                    t0, n = TBS[tbi]
                    dn.norm_mod(l, tbi, 2, 3)
                    for c0 in range(0, 4 * D, 512):
                        wi = dn.load_wblk(w_g["mlp_up"][l][:, c0:c0 + 512], ("mlp_up", l))
                        wb = dn.wblk[wi]
                        for m0 in range(0, 512, 128):
                            pi = nps()
                            for kc in range(KC):
                                P.op("pe", lambda e: e.matmul(ps[pi][:, :n], wb[:, kc, m0:m0 + 128], hT[:, kc, :n],
                                                              start=(kc == 0), stop=(kc == KC - 1)),
                                     reads=["hT", ("wblk", wi)], writes=[psk[pi]])
                            si = dn.nstg()
                            P.op("dve", lambda e: e.tensor_scalar(out=dn.stg[si][:, :n], in0=ps[pi][:, :n], scalar1=0.0,
                                                                  scalar2=None, op0=ALU.max), reads=[psk[pi]], writes=[("stg", si)])
                            P.op("act", lambda e: e.activation(out=hid[:, (c0 + m0) // 128, :n], in_=dn.stg[si][:, :n], func=AF.Square),
                                 reads=[("stg", si)], writes=["hid"])
                    yb = dn.xt
                    for c0 in range(0, D, 512):
                        pis = [nps() for _ in range(4)]
                        for kq in range(4):
                            wi = dn.load_wblk(w_g["mlp_down"][l][kq * 2048:(kq + 1) * 2048, c0:c0 + 512], ("mlp_down", l))
                            wb = dn.wblk[wi]
                            for j in range(4):
                                pi = pis[j]
                                for kc in range(KC):
                                    P.op("pe", lambda e: e.matmul(ps[pi][:, :n], wb[:, kc, j * 128:(j + 1) * 128],
                                                                  hid[:, kq * KC + kc, :n],
                                                                  start=(kq == 0 and kc == 0), stop=(kq == 3 and kc == KC - 1)),
                                         reads=["hid", ("wblk", wi)], writes=[psk[pi]])
                        for j in range(4):
                            dn.evac(yb[:, c0 // 128 + j, :n], pis[j], 128, n, "xt", j)
                    dn.norm_residual(l, tbi, 3)
                P.barrier()
                P.emit()

        def phase_p3(l):
            last = (LS[l] in cfg.get('last_ids', (3,)))
            with contextlib.ExitStack() as ph:
                dn = Dense(ph, with_m=True)
                hT, mT, ybr = dn.hT, dn.mT, dn.ybr
                for tbi in tbs_sel:
                    if last and tbi == 0:
                        continue
                    t0, n = TBS[tbi]
                    dn.norm_mod(l, tbi, 0, 0)
                    for ch in range(18):
                        si = dn.nstg()
                        P.dma("sp", dn.stg[si][:, :n], Y[ch * 128:(ch + 1) * 128, t0:t0 + n], reads=["Y"], writes=[("stg", si)])
                        if ch % 2 == 0:
                            P.op("act", lambda e: e.copy(out=ybr[:, ch, :n], in_=dn.stg[si][:, :n]), reads=[("stg", si)], writes=["ybr"])
                        else:
                            P.op("dve", lambda e: e.tensor_copy(out=ybr[:, ch, :n], in_=dn.stg[si][:, :n]), reads=[("stg", si)], writes=["ybr"])
                    macc = dn.xt
                    for bi, (wn, y0, yc) in enumerate(BR):
                        nkb = yc // 128
                        for c0 in range(0, D, 512):
                            wi = dn.load_wblk(w_g["w_in"][l][:, GATE0 + bi * D + c0:GATE0 + bi * D + c0 + 512], ("w_in", l))
                            wb = dn.wblk[wi]
                            w2i = cnt["c"] % 2
                            cnt["c"] += 1
                            w2 = dn.wb2[w2i]
                            P.dma("sp", w2[:, :nkb, :], w_g[wn][l][:, c0:c0 + 512].rearrange("(k p) c -> p k c", p=128),
                                  reads=[(wn, l)], writes=[("wb2", w2i)])
                            for j in range(4):
                                oc = c0 // 128 + j
                                pg, pb = nps(), nps()
                                for kc in range(KC):
                                    P.op("pe", lambda e: e.matmul(ps[pg][:, :n], wb[:, kc, j * 128:(j + 1) * 128], hT[:, kc, :n],
                                                                  start=(kc == 0), stop=(kc == KC - 1)),
                                         reads=["hT", ("wblk", wi)], writes=[psk[pg]])
                                for kb in range(nkb):
                                    P.op("pe", lambda e: e.matmul(ps[pb][:, :n], w2[:, kb, j * 128:(j + 1) * 128],
                                                                  ybr[:, y0 // 128 + kb, :n],
                                                                  start=(kb == 0), stop=(kb == nkb - 1)),
                                         reads=["ybr", ("wb2", w2i)], writes=[psk[pb]])
                                sgi = cnt["sq"] % 2
                                cnt["sq"] += 1
                                sg = dn.sig[sgi]
                                P.op("act", lambda e: e.activation(out=sg[:, :n], in_=ps[pg][:, :n], func=AF.Sigmoid),
                                     reads=[psk[pg]], writes=[("sig", sgi)])
                                if bi == 0:
                                    P.op("dve", lambda e: e.tensor_tensor(out=macc[:, oc, :n], in0=sg[:, :n], in1=ps[pb][:, :n], op=ALU.mult),
                                         reads=[("sig", sgi), psk[pb]], writes=["xt"])
                                else:
                                    P.op("dve", lambda e: e.tensor_tensor(out=sg[:, :n], in0=sg[:, :n], in1=ps[pb][:, :n], op=ALU.mult),
                                         reads=[("sig", sgi), psk[pb]], writes=[("sig", sgi)])
                                    if bi < 3:
                                        P.op("dve", lambda e: e.tensor_tensor(out=macc[:, oc, :n], in0=macc[:, oc, :n], in1=sg[:, :n], op=ALU.add),
                                             reads=[("sig", sgi), "xt"], writes=["xt"])
                                    else:
                                        P.op("dve", lambda e: e.tensor_tensor(out=mT[:, oc, :n], in0=macc[:, oc, :n], in1=sg[:, :n], op=ALU.add),
                                             reads=[("sig", sgi), "xt"], writes=["mT"])
                    yb = dn.xt
                    for c0 in range(0, D, 512):
                        wi = dn.load_wblk(w_g["w_o"][l][:, c0:c0 + 512], ("w_o", l))
                        wb = dn.wblk[wi]
                        for j in range(4):
                            pi = nps()
                            for kc in range(KC):
                                P.op("pe", lambda e: e.matmul(ps[pi][:, :n], wb[:, kc, j * 128:(j + 1) * 128], mT[:, kc, :n],
                                                              start=(kc == 0), stop=(kc == KC - 1)),
                                     reads=["mT", ("wblk", wi)], writes=[psk[pi]])
                            dn.evac(yb[:, c0 // 128 + j, :n], pi, 128, n, "xt", j)
                    dn.norm_residual(l, tbi, 1)
                P.barrier()
                P.emit()

        def phase_conv(l):
            last = (LS[l] in cfg.get('last_ids', (3,)))
            with contextlib.ExitStack() as ph:
                acc = sbp(ph, "cv_acc", [128, 4, T])
                vv = sbp(ph, "cv_v", [128, T])
                gg = sbp(ph, "cv_g", [128, T])
                cp = sbp(ph, "cv_p", [128, 4, 34])
                st2 = [sbp(ph, "cv_s%d" % i, [128, 512]) for i in range(2)]
                mean = sbp(ph, "cv_mean", [128, 512])
                rs = sbp(ph, "cv_rs", [128, 512])
                P.dma("sp", cp[:], convp_in[:, l, :, :], writes=["cvp"])
                segs = [(256, 32, 64)] + ([] if last else [(0, 1, 256)])
                for j in range(4):
                    P.dma("sp", vv[:], U[CV0 + j * 128:CV0 + (j + 1) * 128, :], reads=["U"], writes=["cv_v"])
                    P.dma("sp", gg[:], U[CV0 + 512 + j * 128:CV0 + 512 + (j + 1) * 128, :], reads=["U"], writes=["cv_g"])
                    P.op("act", lambda e: e.activation(out=gg[:], in_=gg[:], func=AF.Sigmoid), reads=["cv_g"], writes=["cv_g"])
                    P.op("dve", lambda e: e.tensor_tensor(out=vv[:], in0=vv[:], in1=gg[:], op=ALU.mult), reads=["cv_v", "cv_g"], writes=["cv_v"])
                    P.op("dve", lambda e: e.tensor_scalar(out=acc[:, j, :], in0=vv[:], scalar1=cp[:, j, 15:16], scalar2=cp[:, j, 31:32],
                                                          op0=ALU.mult, op1=ALU.add), reads=["cv_v", "cvp"], writes=[("cv_acc", j)])
                    for (s0, ns, sl) in segs:
                        av = acc[:, j, s0:s0 + ns * sl].rearrange("p (s t) -> p s t", t=sl)
                        xv = vv[:, s0:s0 + ns * sl].rearrange("p (s t) -> p s t", t=sl)
                        for k in range(31):
                            o = k - 15
                            if o == 0:
                                continue
                            lo, hi = max(0, -o), sl - max(0, o)
                            P.op("dve", lambda e: e.scalar_tensor_tensor(out=av[:, :, lo:hi], in0=xv[:, :, lo + o:hi + o],
                                                                         scalar=cp[:, j, k:k + 1], in1=av[:, :, lo:hi],
                                                                         op0=ALU.mult, op1=ALU.add),
                                 reads=["cv_v", "cvp", ("cv_acc", j)], writes=[("cv_acc", j)])
                AK = [("cv_acc", j) for j in range(4)]
                for tbi in range(5):
                    if last and tbi == 0:
                        continue
                    t0, n = TBS[tbi]
                    p1, p2 = nps(), nps()
                    for j in range(4):
                        P.op("pe", lambda e: e.matmul(ps[p1][:, :n], ones_f[:], acc[:, j, t0:t0 + n], start=(j == 0), stop=(j == 3)),
                             reads=AK + ["ones_f"], writes=[psk[p1]])
                    for j in range(4):
                        si = j % 2
                        P.op("act", lambda e: e.activation(out=st2[si][:, :n], in_=acc[:, j, t0:t0 + n], func=AF.Square),
                             reads=AK, writes=[("cv_s", si)])
                        P.op("pe", lambda e: e.matmul(ps[p2][:, :n], ones_f[:], st2[si][:, :n], start=(j == 0), stop=(j == 3)),
                             reads=[("cv_s", si), "ones_f"], writes=[psk[p2]])
                    P.op("dve", lambda e: e.tensor_scalar(out=mean[:, :n], in0=ps[p1][:, :n], scalar1=1.0 / 512, scalar2=None, op0=ALU.mult),
                         reads=[psk[p1]], writes=["cv_mean"])
                    P.op("dve", lambda e: e.tensor_tensor(out=rs[:, :n], in0=mean[:, :n], in1=mean[:, :n], op=ALU.mult),
                         reads=["cv_mean"], writes=["cv_rs"])
                    P.op("dve", lambda e: e.scalar_tensor_tensor(out=rs[:, :n], in0=ps[p2][:, :n], scalar=1.0 / 512, in1=rs[:, :n],
                                                                 op0=ALU.mult, op1=ALU.subtract), reads=[psk[p2], "cv_rs"], writes=["cv_rs"])
                    P.op("dve", lambda e: e.tensor_scalar(out=rs[:, :n], in0=rs[:, :n], scalar1=1e-5, scalar2=None, op0=ALU.add),
                         reads=["cv_rs"], writes=["cv_rs"])
                    P.op("act", lambda e: e.activation(out=rs[:, :n], in_=rs[:, :n], func=AF.Sqrt), reads=["cv_rs"], writes=["cv_rs"])
                    P.op("dve", lambda e: e.reciprocal(out=rs[:, :n], in_=rs[:, :n]), reads=["cv_rs"], writes=["cv_rs"])
                    for j in range(4):
                        si = j % 2
                        P.op("dve", lambda e: e.tensor_tensor(out=st2[si][:, :n], in0=acc[:, j, t0:t0 + n], in1=mean[:, :n], op=ALU.subtract),
                             reads=AK + ["cv_mean"], writes=[("cv_s", si)])
                        P.op("dve", lambda e: e.tensor_tensor(out=st2[si][:, :n], in0=st2[si][:, :n], in1=rs[:, :n], op=ALU.mult),
                             reads=[("cv_s", si), "cv_rs"], writes=[("cv_s", si)])
                        P.op("act", lambda e: e.activation(out=st2[si][:, :n], in_=st2[si][:, :n], func=AF.Silu,
                                                           bias=cp[:, j, 33:34], scale=cp[:, j, 32:33]),
                             reads=[("cv_s", si), "cvp"], writes=[("cv_s", si)])
                        P.dma("sp", Y[YC0 + j * 128:YC0 + (j + 1) * 128, t0:t0 + n], st2[si][:, :n], reads=[("cv_s", si)],
                              writes=[("Y", YC0 + j * 128, tbi)])
                P.barrier()
                P.emit()

        def phase_fft(l):
            last = (LS[l] in cfg.get('last_ids', (3,)))
            with contextlib.ExitStack() as ph:
                xtm = sbp(ph, "ff_x", [128, 18, 512], BF16)
                cb = [sbp(ph, "ff_cb%d" % i, [128, KC, 512], BF16) for i in range(2)]
                small_f = sbp(ph, "ff_smf", [128, 2 * 128 + 4 * 256])
                small = sbp(ph, "ff_sm", [128, 2 * 128 + 4 * 256], BF16)
                z = [sbp(ph, "ff_z%d" % i, [128, 512], BF16) for i in range(4)]
                st2 = [sbp(ph, "ff_s%d" % i, [128, 512]) for i in range(2)]
                P.dma("sp", small_f[:], dft_small_in, writes=["ff_smf"])
                P.op("dve", lambda e: e.tensor_copy(out=small[:], in_=small_f[:]), reads=["ff_smf"], writes=["ff_sm"])
                Cd, nSd = small[:, 0:128], small[:, 128:256]
                C256 = small[:, 256:768].rearrange("p (l t) -> p l t", t=256)
                S256 = small[:, 768:1280].rearrange("p (l t) -> p l t", t=256)
                P.dma("sp", xtm[:], Xtm.rearrange("(a p) c -> p a c", p=128), reads=["Xtm"], writes=["ff_x"])
                zc = 0
                jobs = [(256 + lb * 512, 512, 2, 16, lb) for lb in range(4)] + ([] if last else [(0, 256, 0, 2, None)])
                for (t0, n, tile0, ntile, lb) in jobs:
                    if lb is not None:
                        P.dma("sp", cb[0][:], dft_g["dftC"][:, lb * 512:(lb + 1) * 512].rearrange("(k p) c -> p k c", p=128),
                              reads=["dftC"], writes=[("ff_cb", 0)])
                        P.dma("sp", cb[1][:], dft_g["dftS"][:, lb * 512:(lb + 1) * 512].rearrange("(k p) c -> p k c", p=128),
                              reads=["dftS"], writes=[("ff_cb", 1)])
                    scale = 1.0 / float(np.sqrt((2048 if lb is not None else 256) * 128.0))
                    for g in range(4):
                        zs = []
                        for which in range(2):
                            pi = nps()
                            for lt in range(ntile):
                                rhs = (cb[which][:, lt, :] if lb is not None else (C256 if which == 0 else S256)[:, lt, :])
                                P.op("pe", lambda e: e.matmul(ps[pi][:, :n], xtm[:, tile0 + lt, g * 128:(g + 1) * 128], rhs,
                                                              start=(lt == 0), stop=(lt == ntile - 1)),
                                     reads=["ff_x", ("ff_cb", which), "ff_sm"], writes=[psk[pi]])
                            zi = zc % 4
                            zc += 1
                            P.op("act", lambda e: e.activation(out=z[zi][:, :n], in_=ps[pi][:, :n], func=AF.Identity),
                                 reads=[psk[pi]], writes=[("ff_z", zi)])
                            zs.append(zi)
                        pi = nps()
                        P.op("pe", lambda e: e.matmul(ps[pi][:, :n], Cd, z[zs[0]][:, :n], start=True, stop=False),
                             reads=["ff_sm", ("ff_z", zs[0])], writes=[psk[pi]])
                        P.op("pe", lambda e: e.matmul(ps[pi][:, :n], nSd, z[zs[1]][:, :n], start=False, stop=True),
                             reads=["ff_sm", ("ff_z", zs[1])], writes=[psk[pi]])
                        si = g % 2
                        P.op("dve", lambda e: e.tensor_scalar(out=st2[si][:, :n], in0=ps[pi][:, :n], scalar1=scale, scalar2=None, op0=ALU.mult),
                             reads=[psk[pi]], writes=[("ff_s", si)])
                        P.dma("sp", Y[YF0 + g * 128:YF0 + (g + 1) * 128, t0:t0 + n], st2[si][:, :n], reads=[("ff_s", si)],
                              writes=[("Y", YF0 + g * 128, t0)])
                P.barrier()
                P.emit()


        def phase_ssd(l):
            last = (LS[l] in cfg.get('last_ids', (3,)))
            NCH = T // 128
            with contextlib.ExitStack() as ph:
                xbc = sbp(ph, "sd_xbc", [128, 14, T], BF16)
                vtmp = sbp(ph, "sd_v", [128, T])
                atmp = sbp(ph, "sd_a", [128, T])
                cp = sbp(ph, "sd_cp", [128, 14, 6])
                sv = sbp(ph, "sd_sv", [128, 60])
                c3 = sbp(ph, "sd_c3", [128, 3, 128])
                identb = sbp(ph, "sd_id", [128, 128], BF16)
                P.dma("sp", cp[:], ssdp_in[:, l, :, :], writes=["sd_cp"])
                P.dma("sp", sv[:], ssdv_in[:, l, :], writes=["sd_sv"])
                P.dma("sp", c3[:], cst128_in, writes=["sd_c3"])
                P.op("dve", lambda e: e.tensor_copy(out=identb[:], in_=c3[:, 0, :]), reads=["sd_c3"], writes=["sd_id"])
                TRI = [c3[:, 1, :], c3[:, 2, :]]
                for j in range(14):
                    P.dma("sp", vtmp[:], U[768 + j * 128:768 + (j + 1) * 128, :], reads=["U"], writes=["sd_v"])
                    P.op("dve", lambda e: e.tensor_scalar(out=atmp[:], in0=vtmp[:], scalar1=cp[:, j, 2:3], scalar2=cp[:, j, 5:6],
                                                          op0=ALU.mult, op1=ALU.add), reads=["sd_v", "sd_cp"], writes=["sd_a"])
                    for (s0, sl) in ((0, 256), (256, 2048)):
                        for k in (0, 1, 3, 4):
                            o = k - 2
                            lo, hi = s0 + max(0, -o), s0 + sl - max(0, o)
                            P.op("dve", lambda e: e.scalar_tensor_tensor(out=atmp[:, lo:hi], in0=vtmp[:, lo + o:hi + o],
                                                                         scalar=cp[:, j, k:k + 1], in1=atmp[:, lo:hi],
                                                                         op0=ALU.mult, op1=ALU.add),
                                 reads=["sd_v", "sd_cp", "sd_a"], writes=["sd_a"])
                    P.op("act", lambda e: e.activation(out=xbc[:, j, :], in_=atmp[:], func=AF.Silu), reads=["sd_a"], writes=[("sd_xbc", j)])
                XK = [("sd_xbc", j) for j in range(14)]
                dt_tm = sbp(ph, "sd_dt", [128, NCH, 24])
                a_tm = sbp(ph, "sd_atm", [128, NCH, 24])
                ncum = sbp(ph, "sd_ncum", [128, NCH, 24])
                negA = sbp(ph, "sd_negA", [128, 24])
                P.op("act", lambda e: e.activation(out=negA[:], in_=sv[:, 24:48], func=AF.Exp), reads=["sd_sv"], writes=["sd_negA"])
                P.op("dve", lambda e: e.tensor_scalar(out=negA[:], in0=negA[:], scalar1=-1.0, scalar2=None, op0=ALU.mult),
                     reads=["sd_negA"], writes=["sd_negA"])
                P.dma("sp", dt_tm[:], DTtm.rearrange("(a p) c -> p a c", p=128), reads=["DTtm"], writes=["sd_dt"])
                for c in range(NCH):
                    P.op("dve", lambda e: e.tensor_tensor(out=dt_tm[:, c, :], in0=dt_tm[:, c, :], in1=sv[:, 0:24], op=ALU.add),
                         reads=["sd_dt", "sd_sv"], writes=["sd_dt"])
                P.op("act", lambda e: e.activation(out=dt_tm[:], in_=dt_tm[:], func=AF.Exp), reads=["sd_dt"], writes=["sd_dt"])
                P.op("act", lambda e: e.activation(out=dt_tm[:], in_=dt_tm[:], func=AF.Ln, bias=1.0), reads=["sd_dt"], writes=["sd_dt"])
                for c in range(NCH):
                    P.op("dve", lambda e: e.tensor_tensor(out=a_tm[:, c, :], in0=dt_tm[:, c, :], in1=negA[:], op=ALU.mult),
                         reads=["sd_dt", "sd_negA"], writes=["sd_atm"])
                for c in range(NCH):
                    pi = nps()
                    for d in range(2):
                        P.op("pe", lambda e: e.matmul(ps[pi][:, d * 12:(d + 1) * 12], TRI[d], a_tm[:, c, d * 12:(d + 1) * 12],
                                                      start=True, stop=True), reads=["sd_c3", "sd_atm"], writes=[psk[pi]])
                    P.op("dve", lambda e: e.tensor_scalar(out=ncum[:, c, :], in0=ps[pi][:, 0:24], scalar1=-1.0, scalar2=None, op0=ALU.mult),
                         reads=[psk[pi]], writes=["sd_ncum"])
                xs_tm = sbp(ph, "sd_xstm", [128, NCH, 768], BF16)
                B_tm = sbp(ph, "sd_btm", [128, NCH, 512], BF16)
                CBt = sbp(ph, "sd_cbt", [128, NCH, 4, 128], BF16)
                for c in range(NCH):
                    tsl = slice(c * 128, (c + 1) * 128)
                    for (dst, j0, nj) in ((xs_tm, 0, 6), (B_tm, 6, 4)):
                        for jb in range(0, nj, 4):
                            pi = nps()
                            jn = min(4, nj - jb)
                            for jj in range(jn):
                                P.op("pe", lambda e: e.matmul(ps[pi][:, jj * 128:(jj + 1) * 128], xbc[:, j0 + jb + jj, tsl], identb[:],
                                                              start=True, stop=True), reads=XK + ["sd_id"], writes=[psk[pi]])
                            P.op("act", lambda e: e.activation(out=dst[:, c, jb * 128:(jb + jn) * 128], in_=ps[pi][:, :jn * 128], func=AF.Identity),
                                 reads=[psk[pi]], writes=["sd_tm"])
                    pi = nps()
                    for g in range(4):
                        P.op("pe", lambda e: e.matmul(ps[pi][:, g * 128:(g + 1) * 128], xbc[:, 6 + g, tsl], xbc[:, 10 + g, tsl],
                                                      start=True, stop=True), reads=XK, writes=[psk[pi]])
                    P.op("dve", lambda e: e.tensor_scalar(out=CBt[:, c, :, :].rearrange("p g t -> p (g t)"), in0=ps[pi][:, :512],
                                                          scalar1=1.0, scalar2=None, op0=ALU.mult), reads=[psk[pi]], writes=["sd_cbt"])
                H = sbp(ph, "sd_H", [128, 24, 64])
                Hb = sbp(ph, "sd_Hb", [128, 24, 64], BF16)
                P.op("dve", lambda e: e.memset(H[:], 0.0), writes=["sd_H"])
                P.op("dve", lambda e: e.memset(Hb[:], 0.0), writes=["sd_Hb"])
                mcb = sbp(ph, "sd_mcb", [128, 4, 128], BF16)
                aTri = [sbp(ph, "sd_aTri%d" % i, [128, 128]) for i in range(2)]
                Dm = [sbp(ph, "sd_D%d" % i, [128, 128]) for i in range(2)]
                ec = [sbp(ph, "sd_ec%d" % i, [128, 128]) for i in range(2)]
                Mt = [sbp(ph, "sd_Mt%d" % i, [128, 128], BF16) for i in range(2)]
                Cd_ = [sbp(ph, "sd_Cd%d" % i, [128, 128], BF16) for i in range(2)]
                xd = [sbp(ph, "sd_xd%d" % i, [128, 64], BF16) for i in range(2)]
                xdd = [sbp(ph, "sd_xdd%d" % i, [128, 64], BF16) for i in range(2)]
                decc = [sbp(ph, "sd_dec%d" % i, [128, 1]) for i in range(2)]
                lastc = [sbp(ph, "sd_lastc%d" % i, [128, 1]) for i in range(2)]
                ytile = [sbp(ph, "sd_y%d" % i, [128, 128]) for i in range(2)]
                yfin = sbp(ph, "sd_yfin", [128, 6, 128])
                zt = sbp(ph, "sd_z", [128, 6, 128])
                sqt = sbp(ph, "sd_sq", [128, 128])
                rsd = sbp(ph, "sd_rs", [128, 128])
                u = 0
                for d in range(2):
                    order = list(range(NCH)) if d == 0 else [1, 0] + list(range(NCH - 1, 1, -1))
                    lastcol = 127 if d == 0 else 0
                    for c in order:
                        tsl = slice(c * 128, (c + 1) * 128)
                        for g in range(4):
                            P.op("dve", lambda e: e.tensor_tensor(out=mcb[:, g, :], in0=CBt[:, c, g, :], in1=TRI[d], op=ALU.mult),
                                 reads=["sd_cbt", "sd_c3"], writes=["sd_mcb"])
                        if d == 1:
                            P.dma("sp", zt[:], U[0:768, tsl].rearrange("(k p) t -> p k t", p=128), reads=["U"], writes=["sd_z"])
                        for hp in range(6):
                            pyi = nps()
                            for hh in range(2):
                                h = hp * 2 + hh
                                g = h // 3
                                col = d * 12 + h
                                b = u % 2
                                u += 1
                                P.op("dve", lambda e: e.tensor_scalar(out=aTri[b][:], in0=TRI[d], scalar1=a_tm[:, c, col:col + 1], scalar2=None,
                                                                      op0=ALU.mult), reads=["sd_c3", "sd_atm"], writes=[("sd_aTri", b)])
                                pb = nps()
                                P.op("pe", lambda e: e.matmul(ps[pb][:, :128], ones_f[:], aTri[b][:], start=True, stop=True),
                                     reads=["ones_f", ("sd_aTri", b)], writes=[psk[pb]])
                                P.op("dve", lambda e: e.tensor_scalar(out=Dm[b][:], in0=ps[pb][:, :128], scalar1=ncum[:, c, col:col + 1],
                                                                      scalar2=0.0, op0=ALU.add, op1=ALU.min),
                                     reads=[psk[pb], "sd_ncum"], writes=[("sd_D", b)])
                                P.op("act", lambda e: e.activation(out=Dm[b][:], in_=Dm[b][:], func=AF.Exp), reads=[("sd_D", b)], writes=[("sd_D", b)])
                                P.op("dve", lambda e: e.tensor_tensor(out=Mt[b][:], in0=Dm[b][:], in1=mcb[:, g, :], op=ALU.mult),
                                     reads=[("sd_D", b), "sd_mcb"], writes=[("sd_Mt", b)])
                                P.op("act", lambda e: e.activation(out=ec[b][:], in_=ps[pb][:, :128], func=AF.Exp), reads=[psk[pb]], writes=[("sd_ec", b)])
                                P.op("dve", lambda e: e.tensor_tensor(out=Cd_[b][:], in0=xbc[:, 10 + g, tsl], in1=ec[b][:], op=ALU.mult),
                                     reads=XK + [("sd_ec", b)], writes=[("sd_Cd", b)])
                                P.op("dve", lambda e: e.tensor_scalar(out=xd[b][:], in0=xs_tm[:, c, h * 64:(h + 1) * 64],
                                                                      scalar1=dt_tm[:, c, col:col + 1], scalar2=None, op0=ALU.mult),
                                     reads=["sd_tm", "sd_dt"], writes=[("sd_xd", b)])
                                yo = ps[pyi][hh * 64:(hh + 1) * 64, :128]
                                P.op("pe", lambda e: e.matmul(yo, xd[b][:], Mt[b][:], start=True, stop=False),
                                     reads=[("sd_xd", b), ("sd_Mt", b)], writes=[psk[pyi]])
                                P.op("pe", lambda e: e.matmul(yo, Hb[:, col, :], Cd_[b][:], start=False, stop=True),
                                     reads=["sd_Hb", ("sd_Cd", b)], writes=[psk[pyi]])
                                P.op("dve", lambda e: e.tensor_scalar(out=lastc[b][:], in0=ps[pb][:, lastcol:lastcol + 1], scalar1=1.0, scalar2=None,
                                                                      op0=ALU.mult), reads=[psk[pb]], writes=[("sd_lastc", b)])
                                P.op("act", lambda e: e.activation(out=decc[b][:], in_=ncum[:, c, col:col + 1], func=AF.Exp,
                                                                   bias=lastc[b][:, 0:1]),
                                     reads=[("sd_lastc", b), "sd_ncum"], writes=[("sd_dec", b)])
                                P.op("dve", lambda e: e.tensor_scalar(out=xdd[b][:], in0=xd[b][:], scalar1=decc[b][:, 0:1], scalar2=None, op0=ALU.mult),
                                     reads=[("sd_xd", b), ("sd_dec", b)], writes=[("sd_xdd", b)])
                                psi = nps()
                                P.op("pe", lambda e: e.matmul(ps[psi][:, :64], B_tm[:, c, g * 128:(g + 1) * 128], xdd[b][:], start=True, stop=True),
                                     reads=["sd_tm", ("sd_xdd", b)], writes=[psk[psi]])
                                P.op("dve", lambda e: e.scalar_tensor_tensor(out=H[:, col, :], in0=H[:, col, :], scalar=ec[b][:, lastcol:lastcol + 1],
                                                                             in1=ps[psi][:, :64], op0=ALU.mult, op1=ALU.add),
                                     reads=["sd_H", ("sd_ec", b), psk[psi]], writes=["sd_H"])
                                P.op("act", lambda e: e.copy(out=Hb[:, col, :], in_=H[:, col, :]), reads=["sd_H"], writes=["sd_Hb"])
                            yb_ = u % 2
                            yt = ytile[yb_]
                            rows = slice(hp * 128, (hp + 1) * 128)
                            if d == 0:
                                P.op("act", lambda e: e.activation(out=yt[:], in_=ps[pyi][:, :128], func=AF.Identity), reads=[psk[pyi]], writes=[("sd_y", yb_)])
                                P.dma("sp", Ytmp[rows, tsl], yt[:], reads=[("sd_y", yb_)], writes=[("Ytmp", hp, c)])
                            else:
                                P.dma("sp", yt[:], Ytmp[rows, tsl], reads=["Ytmp", ("Ytmp", hp, c)], writes=[("sd_y", yb_)])
                                P.op("dve", lambda e: e.tensor_tensor(out=yt[:], in0=yt[:], in1=ps[pyi][:, :128], op=ALU.add),
                                     reads=[("sd_y", yb_), psk[pyi]], writes=[("sd_y", yb_)])
                                P.op("dve", lambda e: e.scalar_tensor_tensor(out=yt[:], in0=xbc[:, hp, tsl], scalar=sv[:, 48 + hp:49 + hp], in1=yt[:],
                                                                             op0=ALU.mult, op1=ALU.add), reads=XK + ["sd_sv", ("sd_y", yb_)], writes=[("sd_y", yb_)])
                                P.op("act", lambda e: e.activation(out=zt[:, hp, :], in_=zt[:, hp, :], func=AF.Silu), reads=["sd_z"], writes=["sd_z"])
                                P.op("dve", lambda e: e.tensor_tensor(out=yfin[:, hp, :], in0=yt[:], in1=zt[:, hp, :], op=ALU.mult),
                                     reads=[("sd_y", yb_), "sd_z"], writes=["sd_yfin"])
                        if d == 1 and not (last and c < 2):
                            pn = nps()
                            for hp in range(6):
                                P.op("act", lambda e: e.activation(out=sqt[:], in_=yfin[:, hp, :], func=AF.Square), reads=["sd_yfin"], writes=["sd_sq"])
                                P.op("pe", lambda e: e.matmul(ps[pn][:, :128], ones_f[:], sqt[:], start=(hp == 0), stop=(hp == 5)),
                                     reads=["ones_f", "sd_sq"], writes=[psk[pn]])
                            P.op("dve", lambda e: e.tensor_scalar(out=rsd[:], in0=ps[pn][:, :128], scalar1=1.0 / 768, scalar2=RMS_EPS,
                                                                  op0=ALU.mult, op1=ALU.add), reads=[psk[pn]], writes=["sd_rs"])
                            P.op("act", lambda e: e.activation(out=rsd[:], in_=rsd[:], func=AF.Sqrt), reads=["sd_rs"], writes=["sd_rs"])
                            P.op("dve", lambda e: e.reciprocal(out=rsd[:], in_=rsd[:]), reads=["sd_rs"], writes=["sd_rs"])
                            for hp in range(6):
                                yb_ = hp % 2
                                P.op("dve", lambda e: e.scalar_tensor_tensor(out=ytile[yb_][:], in0=yfin[:, hp, :], scalar=sv[:, 54 + hp:55 + hp],
                                                                             in1=rsd[:], op0=ALU.mult, op1=ALU.mult),
                                     reads=["sd_yfin", "sd_sv", "sd_rs"], writes=[("sd_y", yb_)])
                                P.dma("sp", Y[YS0 + hp * 128:YS0 + (hp + 1) * 128, tsl], ytile[yb_][:], reads=[("sd_y", yb_)],
                                      writes=[("Y", YS0 + hp * 128, c)])
                P.barrier()
                P.emit()


        def phase_rwkv(l):
            last = (LS[l] in cfg.get('last_ids', (3,)))
            NCH = T // 128
            LAT = slice(NCTX, T)

            def perm_in(ap):
                return ap.rearrange("p (r c) -> p c r", c=64)

            with contextlib.ExitStack() as ph:
                gmall = sbp(ph, "rw_gm", [128, 2, 4, NCH])
                eglall = sbp(ph, "rw_egl", [128, 2, 4, NCH])
                with contextlib.ExitStack() as ph1:
                    rp = sbp(ph1, "rw_p", [128, 4, 9])
                    rmu = sbp(ph1, "rw_mu", [128, 15])
                    rww = sbp(ph1, "rw_w", [128, 3, 512])
                    blk = sbp(ph1, "rw_blk", [128, 128])
                    P.dma("sp", rp[:], rwp_in[:, l, :, :], writes=["rw_p"])
                    P.dma("sp", rmu[:], rwmu_in[:, l, :], writes=["rw_mu"])
                    P.dma("sp", rww[:], rww_in[:, l, :, :], writes=["rw_w"])
                    P.dma("sp", blk[:], blk_in, writes=["rw_blk"])
                    uraw = sbp(ph1, "rw_uraw", [128, T])
                    up = sbp(ph1, "rw_up", [128, T])
                    sh = sbp(ph1, "rw_sh", [128, T])
                    LW = sbp(ph1, "rw_LW", [128, T])
                    AL = sbp(ph1, "rw_AL", [128, T])
                    GL = sbp(ph1, "rw_GL", [128, T])
                    kq = sbp(ph1, "rw_k", [128, T])
                    rq = sbp(ph1, "rw_r", [128, T])
                    vq = sbp(ph1, "rw_v", [128, T])
                    aq = sbp(ph1, "rw_a", [128, T])
                    t1 = sbp(ph1, "rw_t1", [128, T])
                    t2 = sbp(ph1, "rw_t2", [128, T])
                    lwd = [sbp(ph1, "rw_lw%d" % i, [128, T]) for i in range(2)]
                    cmid = sbp(ph1, "rw_cmid", [128, NCH])

                    def mix_chunk(row0, nrow, muc, dst, dkey):
                        P.dma("sp", uraw[:nrow, :], U[RW0 + row0:RW0 + row0 + nrow, :], reads=["U"], writes=["rw_uraw"])
                        P.op("act", lambda e: e.copy(out=up[:nrow, 0:NCTX], in_=uraw[:nrow, 0:NCTX]), reads=["rw_uraw"], writes=["rw_up"])
                        P.op("dve", lambda e: e.tensor_copy(out=up[:nrow, LAT].rearrange("p (c r) -> p c r", r=32), in_=perm_in(uraw[:nrow, LAT])),
                             reads=["rw_uraw"], writes=["rw_up"])
                        for (s0, sl) in ((0, NCTX), (NCTX, NLAT)):
                            P.op("dve", lambda e: e.memset(sh[:nrow, s0:s0 + 1], 0.0), writes=["rw_sh"])
                            P.op("act", lambda e: e.copy(out=sh[:nrow, s0 + 1:s0 + sl], in_=up[:nrow, s0:s0 + sl - 1]), reads=["rw_up"], writes=["rw_sh"])
                            P.op("dve", lambda e: e.tensor_tensor(out=sh[:nrow, s0:s0 + sl - 1], in0=sh[:nrow, s0:s0 + sl - 1],
                                                                  in1=up[:nrow, s0 + 1:s0 + sl], op=ALU.add), reads=["rw_up", "rw_sh"], writes=["rw_sh"])
                        P.op("dve", lambda e: e.scalar_tensor_tensor(out=sh[:nrow, :], in0=sh[:nrow, :], scalar=0.5, in1=up[:nrow, :],
                                                                     op0=ALU.mult, op1=ALU.subtract), reads=["rw_up", "rw_sh"], writes=["rw_sh"])
                        P.op("dve", lambda e: e.scalar_tensor_tensor(out=dst[:nrow, :], in0=sh[:nrow, :], scalar=rmu[:nrow, muc:muc + 1], in1=up[:nrow, :],
                                                                     op0=ALU.mult, op1=ALU.add), reads=["rw_up", "rw_sh", "rw_mu"], writes=[dkey])

                    mix_chunk(1536, 128, 12, LW, "rw_LW")
                    P.op("act", lambda e: e.activation(out=LW[:], in_=LW[:], func=AF.Tanh), reads=["rw_LW"], writes=["rw_LW"])
                    mix_chunk(1664, 64, 13, AL, "rw_AL")
                    mix_chunk(1728, 128, 14, GL, "rw_GL")
                    P.op("act", lambda e: e.activation(out=GL[:], in_=GL[:], func=AF.Sigmoid), reads=["rw_GL"], writes=["rw_GL"])

                    def lora(dst, dkey, lhsT, rhs, rkey, func, bias, post_scale=None):
                        for tbi in range(5):
                            t0, n = TBS[tbi]
                            pi = nps()
                            P.op("pe", lambda e: e.matmul(ps[pi][:, :n], lhsT, rhs[:, t0:t0 + n], start=True, stop=True),
                                 reads=["rw_w", rkey], writes=[psk[pi]])
                            if func is None:
                                P.op("dve", lambda e: e.tensor_scalar(out=dst[:, t0:t0 + n], in0=ps[pi][:, :n], scalar1=1.0, scalar2=None, op0=ALU.mult),
                                     reads=[psk[pi]], writes=[dkey])
                            else:
                                P.op("act", lambda e: e.activation(out=dst[:, t0:t0 + n], in_=ps[pi][:, :n], func=func, bias=bias),
                                     reads=[psk[pi], "rw_p"], writes=[dkey])
                        if post_scale is not None:
                            P.op("dve", lambda e: e.tensor_scalar(out=dst[:], in0=dst[:], scalar1=post_scale, scalar2=None, op0=ALU.mult),
                                 reads=[dkey], writes=[dkey])

                    def store(dst_rows, src, skey, wkey):
                        P.dma("sp", dst_rows, src[:], reads=[skey], writes=[wkey])

                    for hp in range(4):
                        cs = slice(hp * 128, (hp + 1) * 128)
                        mix_chunk(hp * 128, 128, hp, rq, "rw_r")
                        mix_chunk(512 + hp * 128, 128, 4 + hp, kq, "rw_k")
                        mix_chunk(1024 + hp * 128, 128, 8 + hp, vq, "rw_v")
                        store(RWv[0, cs, :], vq, "rw_v", ("RWv", 0, hp))
                        lora(aq, "rw_a", rww[0:64, 1, cs], AL[0:64, :], "rw_AL", AF.Sigmoid, rp[:, hp, 5:6])
                        lora(t1, "rw_t1", rww[:, 2, cs], GL, "rw_GL", None, None)
                        store(RWv[1, cs, :], t1, "rw_t1", ("RWv", 1, hp))
                        lora(lwd[0], ("rw_lw", 0), rww[0:64, 0, cs], LW[0:64, :], "rw_LW", AF.Sigmoid, rp[:, hp, 6:7], -0.6065306597126334)
                        lora(lwd[1], ("rw_lw", 1), rww[64:128, 0, cs], LW[64:128, :], "rw_LW", AF.Sigmoid, rp[:, hp, 7:8], -0.6065306597126334)
                        P.op("dve", lambda e: e.tensor_scalar(out=t1[:], in0=kq[:], scalar1=rp[:, hp, 0:1], scalar2=None, op0=ALU.mult),
                             reads=["rw_k", "rw_p", "rw_t1"], writes=["rw_t1"])
                        P.op("act", lambda e: e.activation(out=t2[:], in_=t1[:], func=AF.Square), reads=["rw_t1"], writes=["rw_t2"])
                        for tbi in range(5):
                            t0, n = TBS[tbi]
                            pi = nps()
                            P.op("pe", lambda e: e.matmul(ps[pi][:, :n], blk[:], t2[:, t0:t0 + n], start=True, stop=True),
                                 reads=["rw_blk", "rw_t2"], writes=[psk[pi]])
                            P.op("dve", lambda e: e.tensor_scalar(out=sh[:, t0:t0 + n], in0=ps[pi][:, :n], scalar1=1e-12, scalar2=None, op0=ALU.add),
                                 reads=[psk[pi]], writes=["rw_sh"])
                        P.op("act", lambda e: e.activation(out=sh[:], in_=sh[:], func=AF.Sqrt), reads=["rw_sh"], writes=["rw_sh"])
                        P.op("dve", lambda e: e.reciprocal(out=sh[:], in_=sh[:]), reads=["rw_sh"], writes=["rw_sh"])
                        P.op("dve", lambda e: e.tensor_tensor(out=t1[:], in0=t1[:], in1=sh[:], op=ALU.mult), reads=["rw_t1", "rw_sh"], writes=["rw_t1"])
                        P.op("dve", lambda e: e.tensor_tensor(out=t2[:], in0=t1[:], in1=aq[:], op=ALU.mult), reads=["rw_t1", "rw_a"], writes=["rw_t2"])
                        P.op("dve", lambda e: e.tensor_scalar(out=t1[:], in0=t1[:], scalar1=-1.0, scalar2=None, op0=ALU.mult), reads=["rw_t1"], writes=["rw_t1"])
                        P.op("dve", lambda e: e.tensor_scalar(out=aq[:], in0=aq[:], scalar1=-1.0, scalar2=rp[:, hp, 1:2], op0=ALU.add, op1=ALU.mult),
                             reads=["rw_a", "rw_p", "rw_t2"], writes=["rw_a"])
                        P.op("dve", lambda e: e.scalar_tensor_tensor(out=kq[:], in0=aq[:], scalar=1.0, in1=kq[:], op0=ALU.add, op1=ALU.mult),
                             reads=["rw_a", "rw_k"], writes=["rw_k"])
                        P.op("dve", lambda e: e.scalar_tensor_tensor(out=aq[:], in0=rq[:], scalar=rp[:, hp, 2:3], in1=kq[:], op0=ALU.mult, op1=ALU.mult),
                             reads=["rw_r", "rw_k", "rw_p", "rw_a"], writes=["rw_a"])
                        store(RWv[2, cs, :], aq, "rw_a", ("RWv", 2, hp))
                        for d in range(2):
                            src, dst = lwd[d], sh
                            lv = lambda ap: ap.rearrange("p (c t) -> p c t", t=128)
                            cur, other = src, sh
                            ckey = {id(lwd[0]): ("rw_lw", 0), id(lwd[1]): ("rw_lw", 1), id(sh): "rw_sh", id(up): "rw_up"}
                            P.op("act", lambda e: e.copy(out=up[:], in_=src[:]), reads=[("rw_lw", d), "rw_up"], writes=["rw_up"])
                            cur, other = up, sh
                            j = 1
                            while j < 128:
                                cv, ov = lv(cur[:]), lv(other[:])
                                if d == 0:
                                    P.op("act", lambda e: e.copy(out=ov[:, :, 0:j], in_=cv[:, :, 0:j]), reads=[ckey[id(cur)]], writes=[ckey[id(other)]])
                                    P.op("dve", lambda e: e.tensor_tensor(out=ov[:, :, j:], in0=cv[:, :, j:], in1=cv[:, :, :128 - j], op=ALU.add),
                                         reads=[ckey[id(cur)]], writes=[ckey[id(other)]])
                                else:
                                    P.op("act", lambda e: e.copy(out=ov[:, :, 128 - j:], in_=cv[:, :, 128 - j:]), reads=[ckey[id(cur)]], writes=[ckey[id(other)]])
                                    P.op("dve", lambda e: e.tensor_tensor(out=ov[:, :, :128 - j], in0=cv[:, :, :128 - j], in1=cv[:, :, j:], op=ALU.add),
                                         reads=[ckey[id(cur)]], writes=[ckey[id(other)]])
                                cur, other = other, cur
                                j *= 2
                            cum, ck = cur, ckey[id(cur)]
                            oth, ok = other, ckey[id(other)]
                            lastcol = 127 if d == 0 else 0
                            P.op("dve", lambda e: e.tensor_copy(out=cmid[:], in_=lv(cum[:])[:, :, 64]), reads=[ck], writes=["rw_cmid"])
                            for c in range(NCH):
                                P.op("dve", lambda e: e.tensor_scalar(out=cum[:, c * 128:(c + 1) * 128], in0=cum[:, c * 128:(c + 1) * 128],
                                                                      scalar1=cmid[:, c:c + 1], scalar2=None, op0=ALU.subtract),
                                     reads=[ck, "rw_cmid"], writes=[ck])
                            P.op("act", lambda e: e.activation(out=gmall[:, d, hp, :], in_=cmid[:], func=AF.Exp), reads=["rw_cmid"], writes=["rw_gm"])
                            P.op("act", lambda e: e.activation(out=eglall[:, d, hp, :], in_=lv(cum[:])[:, :, lastcol], func=AF.Exp), reads=[ck], writes=["rw_egl"])
                            P.op("dve", lambda e: e.tensor_tensor(out=oth[:], in0=cum[:], in1=lwd[d][:], op=ALU.subtract), reads=[ck, ("rw_lw", d)], writes=[ok])
                            P.op("act", lambda e: e.activation(out=oth[:], in_=oth[:], func=AF.Exp), reads=[ok], writes=[ok])
                            P.op("dve", lambda e: e.tensor_tensor(out=oth[:], in0=oth[:], in1=t1[:], op=ALU.mult), reads=[ok, "rw_t1"], writes=[ok])
                            store(RWo[d, 0, cs, :], oth, ok, ("RWo", d, 0, hp))
                            P.op("act", lambda e: e.activation(out=oth[:], in_=cum[:], func=AF.Exp), reads=[ck, ok], writes=[ok])
                            P.op("dve", lambda e: e.tensor_tensor(out=oth[:], in0=oth[:], in1=rq[:], op=ALU.mult), reads=[ok, "rw_r"], writes=[ok])
                            store(RWo[d, 1, cs, :], oth, ok, ("RWo", d, 1, hp))
                            P.op("act", lambda e: e.activation(out=cum[:], in_=cum[:], func=AF.Exp, scale=-1.0), reads=[ck], writes=[ck])
                            P.op("dve", lambda e: e.tensor_tensor(out=oth[:], in0=cum[:], in1=t2[:], op=ALU.mult), reads=[ck, "rw_t2", ok], writes=[ok])
                            store(RWo[d, 2, cs, :], oth, ok, ("RWo", d, 2, hp))
                            P.op("dve", lambda e: e.tensor_tensor(out=cum[:], in0=cum[:], in1=kq[:], op=ALU.mult), reads=[ck, "rw_k"], writes=[ck])
                            store(RWo[d, 3, cs, :], cum, ck, ("RWo", d, 3, hp))
                    P.barrier()
                    P.emit()

                c3 = sbp(ph, "rw_c3", [128, 3, 128])
                identb = sbp(ph, "rw_idb", [128, 128], BF16)
                blk = sbp(ph, "rw_blk2", [128, 128])
                rp = sbp(ph, "rw_p2", [128, 4, 9])
                mLT = sbp(ph, "rw_mLT", [128, 128])
                mGT = sbp(ph, "rw_mGT", [128, 128])
                P.dma("sp", c3[:], cst128_in, writes=["rw_c3"])
                P.dma("sp", blk[:], blk_in, writes=["rw_blk2"])
                P.dma("sp", rp[:], rwp_in[:, l, :, :], writes=["rw_p2"])
                P.op("dve", lambda e: e.tensor_copy(out=identb[:], in_=c3[:, 0, :]), reads=["rw_c3"], writes=["rw_idb"])
                P.op("dve", lambda e: e.tensor_tensor(out=mLT[:], in0=c3[:, 1, :], in1=c3[:, 0, :], op=ALU.subtract), reads=["rw_c3"], writes=["rw_m"])
                P.op("dve", lambda e: e.tensor_tensor(out=mGT[:], in0=c3[:, 2, :], in1=c3[:, 0, :], op=ALU.subtract), reads=["rw_c3"], writes=["rw_m"])
                identf, mLE, mGE = c3[:, 0, :], c3[:, 1, :], c3[:, 2, :]
                Hs = sbp(ph, "rw_H", [128, 2, 4, 64])
                P.op("dve", lambda e: e.memset(Hs[:], 0.0), writes=["rw_H"])
                Hh = sbp(ph, "rw_Hh", [128, 64])
                Hb = sbp(ph, "rw_Hb", [128, 64], BF16)
                ld = [[sbp(ph, "rw_ld%d_%d" % (q, i), [128, 128]) for i in range(2)] for q in range(5)]
                opb = [[sbp(ph, "rw_ob%d_%d" % (q, i), [128, 128], BF16) for i in range(2)] for q in range(5)]
                Q = [sbp(ph, "rw_Q%d" % i, [128, 128]) for i in range(2)]
                QT = [sbp(ph, "rw_QT%d" % i, [128, 128]) for i in range(2)]
                S = [sbp(ph, "rw_S%d" % i, [128, 128]) for i in range(2)]
                ST = [sbp(ph, "rw_ST%d" % i, [128, 128]) for i in range(2)]
                AKt = sbp(ph, "rw_AKt", [128, 128], BF16)
                RBt = sbp(ph, "rw_RBt", [128, 128], BF16)
                RKt = sbp(ph, "rw_RKt", [128, 128], BF16)
                Vtm = sbp(ph, "rw_Vtm", [128, 64], BF16)
                Btm = sbp(ph, "rw_Btm", [128, 64], BF16)
                Ktm = sbp(ph, "rw_Ktm", [128, 64], BF16)
                W1 = sbp(ph, "rw_W1", [128, 64])
                Ub = sbp(ph, "rw_Ub", [128, 64], BF16)
                yt = [sbp(ph, "rw_y%d" % i, [128, 128]) for i in range(2)]
                fz = [sbp(ph, "rw_f%d" % i, [128, 128]) for i in range(6)]
                RWout = RYtmp
                u = 0
                cnt["psmod"] = 6
                for d in range(2):
                    order = list(range(NCH)) if d == 0 else [1, 0] + list(range(NCH - 1, 1, -1))
                    m_st_strict, m_st_incl, m_ts_strict = (mLT[:], mLE, mGT[:]) if d == 0 else (mGT[:], mGE, mLT[:])
                    for c in order:
                        tsl = slice(c * 128, (c + 1) * 128)
                        for hp in range(4):
                            cs = slice(hp * 128, (hp + 1) * 128)
                            b = u % 2
                            u += 1
                            srcs = [RWo[d, 0, cs, tsl], RWo[d, 1, cs, tsl], RWo[d, 2, cs, tsl], RWo[d, 3, cs, tsl], RWv[0, cs, tsl]]
                            for q in range(5):
                                P.dma("sp", ld[q][b][:], srcs[q], reads=["RWo", "RWv"], writes=[("rw_ld", q, b)])
                                if q % 2 == 0:
                                    P.op("act", lambda e: e.copy(out=opb[q][b][:], in_=ld[q][b][:]), reads=[("rw_ld", q, b)], writes=[("rw_ob", q, b)])
                                else:
                                    P.op("dve", lambda e: e.tensor_copy(out=opb[q][b][:], in_=ld[q][b][:]), reads=[("rw_ld", q, b)], writes=[("rw_ob", q, b)])
                            Ah, Rh, Bh, Kh, Vh = [opb[q][b] for q in range(5)]
                            OK = [("rw_ob", q, b) for q in range(5)]
                            P.op("dve", lambda e: e.tensor_scalar(out=Hh[:], in0=Hs[:, d, hp, :], scalar1=gmall[:, d, hp, c:c + 1], scalar2=None, op0=ALU.mult),
                                 reads=["rw_H", "rw_gm"], writes=["rw_Hh"])
                            P.op("act", lambda e: e.copy(out=Hb[:], in_=Hh[:]), reads=["rw_Hh"], writes=["rw_Hb"])
                            pY, pH = 6, 7
                            for hh in range(2):
                                rs_ = slice(hh * 64, (hh + 1) * 64)
                                def cc_mat(lhs, rhs, mask, dst, dkey):
                                    pi = nps()
                                    P.op("pe", lambda e: e.matmul(ps[pi][:, :128], lhs[rs_, :], rhs[rs_, :], start=True, stop=True), reads=OK, writes=[psk[pi]])
                                    P.op("dve", lambda e: e.tensor_tensor(out=dst[:], in0=ps[pi][:, :128], in1=mask, op=ALU.mult),
                                         reads=[psk[pi], "rw_m", "rw_c3"], writes=[dkey])
                                cc_mat(Bh, Ah, m_st_strict, Q[0], ("rw_Q", 0))
                                cc_mat(Ah, Bh, m_ts_strict, QT[0], ("rw_QT", 0))
                                cc_mat(Kh, Ah, m_st_strict, AKt, "rw_AKt")
                                cc_mat(Bh, Rh, m_st_incl, RBt, "rw_RBt")
                                cc_mat(Kh, Rh, m_st_incl, RKt, "rw_RKt")
                                for (src, dst, dkey) in ((Vh, Vtm, "rw_Vtm"), (Bh, Btm, "rw_Btm"), (Kh, Ktm, "rw_Ktm")):
                                    pi = nps()
                                    P.op("pe", lambda e: e.matmul(ps[pi][:, :64], src[rs_, :], identb[rs_, rs_], start=True, stop=True),
                                         reads=OK + ["rw_idb"], writes=[psk[pi]])
                                    P.op("act", lambda e: e.activation(out=dst[:], in_=ps[pi][:, :64], func=AF.Identity), reads=[psk[pi]], writes=[dkey])
                                P.op("dve", lambda e: e.tensor_tensor(out=S[0][:], in0=Q[0][:], in1=identf, op=ALU.add), reads=[("rw_Q", 0), "rw_c3"], writes=[("rw_S", 0)])
                                P.op("dve", lambda e: e.tensor_tensor(out=ST[0][:], in0=QT[0][:], in1=identf, op=ALU.add), reads=[("rw_QT", 0), "rw_c3"], writes=[("rw_ST", 0)])
                                cur = 0
                                for i in range(1, 7):
                                    nx = 1 - cur
                                    pi = nps()
                                    P.op("pe", lambda e: e.matmul(ps[pi][:, :128], QT[cur][:], Q[cur][:], start=True, stop=True),
                                         reads=[("rw_Q", cur), ("rw_QT", cur)], writes=[psk[pi]])
                                    P.op("act", lambda e: e.activation(out=Q[nx][:], in_=ps[pi][:, :128], func=AF.Identity), reads=[psk[pi]], writes=[("rw_Q", nx)])
                                    if i < 6:
                                        pi2 = nps()
                                        P.op("pe", lambda e: e.matmul(ps[pi2][:, :128], Q[cur][:], QT[cur][:], start=True, stop=True),
                                             reads=[("rw_Q", cur), ("rw_QT", cur)], writes=[psk[pi2]])
                                        P.op("act", lambda e: e.activation(out=QT[nx][:], in_=ps[pi2][:, :128], func=AF.Identity), reads=[psk[pi2]], writes=[("rw_QT", nx)])
                                    pi3 = nps()
                                    P.op("pe", lambda e: e.matmul(ps[pi3][:, :128], ST[cur][:], Q[nx][:], start=True, stop=True),
                                         reads=[("rw_ST", cur), ("rw_Q", nx)], writes=[psk[pi3]])
                                    P.op("dve", lambda e: e.tensor_tensor(out=S[nx][:], in0=S[cur][:], in1=ps[pi3][:, :128], op=ALU.add),
                                         reads=[("rw_S", cur), psk[pi3]], writes=[("rw_S", nx)])
                                    if i < 6:
                                        pi4 = nps()
                                        P.op("pe", lambda e: e.matmul(ps[pi4][:, :128], Q[nx][:], ST[cur][:], start=True, stop=True),
                                             reads=[("rw_ST", cur), ("rw_Q", nx)], writes=[psk[pi4]])
                                        P.op("dve", lambda e: e.tensor_tensor(out=ST[nx][:], in0=ST[cur][:], in1=ps[pi4][:, :128], op=ALU.add),
                                             reads=[("rw_ST", cur), psk[pi4]], writes=[("rw_ST", nx)])
                                    cur = nx
                                Tt = S[cur]
                                pi = nps()
                                P.op("pe", lambda e: e.matmul(ps[pi][:, :64], Ah[rs_, :], Hb[rs_, :], start=True, stop=False), reads=OK + ["rw_Hb"], writes=[psk[pi]])
                                P.op("pe", lambda e: e.matmul(ps[pi][:, :64], AKt[:], Vtm[:], start=False, stop=True), reads=["rw_AKt", "rw_Vtm"], writes=[psk[pi]])
                                P.op("act", lambda e: e.activation(out=W1[:], in_=ps[pi][:, :64], func=AF.Identity), reads=[psk[pi]], writes=["rw_W1"])
                                pi = nps()
                                P.op("pe", lambda e: e.matmul(ps[pi][:, :64], Tt[:], W1[:], start=True, stop=True), reads=[("rw_S", cur), "rw_W1"], writes=[psk[pi]])
                                P.op("act", lambda e: e.activation(out=Ub[:], in_=ps[pi][:, :64], func=AF.Identity), reads=[psk[pi]], writes=["rw_Ub"])
                                yo = ps[pY][rs_, :128]
                                P.op("pe", lambda e: e.matmul(yo, Hb[rs_, :], Rh[rs_, :], start=True, stop=False), reads=OK + ["rw_Hb"], writes=[psk[pY]])
                                P.op("pe", lambda e: e.matmul(yo, Ub[:], RBt[:], start=False, stop=False), reads=["rw_Ub", "rw_RBt"], writes=[psk[pY]])
                                P.op("pe", lambda e: e.matmul(yo, Vtm[:], RKt[:], start=False, stop=True), reads=["rw_Vtm", "rw_RKt"], writes=[psk[pY]])
                                ho = ps[pH][rs_, :64]
                                P.op("pe", lambda e: e.matmul(ho, Btm[:], Ub[:], start=True, stop=False), reads=["rw_Btm", "rw_Ub"], writes=[psk[pH]])
                                P.op("pe", lambda e: e.matmul(ho, Ktm[:], Vtm[:], start=False, stop=True), reads=["rw_Ktm", "rw_Vtm"], writes=[psk[pH]])
                            P.op("dve", lambda e: e.tensor_tensor(out=Hh[:], in0=Hh[:], in1=ps[pH][:, :64], op=ALU.add), reads=["rw_Hh", psk[pH]], writes=["rw_Hh"])
                            P.op("dve", lambda e: e.tensor_scalar(out=Hs[:, d, hp, :], in0=Hh[:], scalar1=eglall[:, d, hp, c:c + 1], scalar2=None, op0=ALU.mult),
                                 reads=["rw_Hh", "rw_egl"], writes=["rw_H"])
                            ytb = yt[b]
                            if d == 0:
                                P.op("act", lambda e: e.activation(out=ytb[:], in_=ps[pY][:, :128], func=AF.Identity), reads=[psk[pY]], writes=[("rw_y", b)])
                                P.dma("sp", RYtmp[cs, tsl], ytb[:], reads=[("rw_y", b)], writes=[("RYtmp", hp, c)])
                            else:
                                P.dma("sp", ytb[:], RYtmp[cs, tsl], reads=[("RYtmp", hp, c)], writes=[("rw_y", b)])
                                P.op("dve", lambda e: e.tensor_tensor(out=ytb[:], in0=ytb[:], in1=ps[pY][:, :128], op=ALU.add), reads=[("rw_y", b), psk[pY]], writes=[("rw_y", b)])
                                f = fz
                                P.dma("sp", f[4][:], RWv[1, cs, tsl], reads=["RWv"], writes=["rw_f4"])
                                P.dma("sp", f[5][:], RWv[2, cs, tsl], reads=["RWv"], writes=["rw_f5"])
                                p1_, p2_, p3_ = nps(), nps(), nps()
                                P.op("pe", lambda e: e.matmul(ps[p1_][:, :128], blk[:], ytb[:], start=True, stop=True), reads=["rw_blk2", ("rw_y", b)], writes=[psk[p1_]])
                                P.op("act", lambda e: e.activation(out=f[0][:], in_=ytb[:], func=AF.Square), reads=[("rw_y", b)], writes=["rw_f0"])
                                P.op("pe", lambda e: e.matmul(ps[p2_][:, :128], blk[:], f[0][:], start=True, stop=True), reads=["rw_blk2", "rw_f0"], writes=[psk[p2_]])
                                P.op("pe", lambda e: e.matmul(ps[p3_][:, :128], blk[:], f[5][:], start=True, stop=True), reads=["rw_blk2", "rw_f5"], writes=[psk[p3_]])
                                P.op("dve", lambda e: e.tensor_scalar(out=f[1][:], in0=ps[p1_][:, :128], scalar1=1.0 / 64, scalar2=None, op0=ALU.mult), reads=[psk[p1_]], writes=["rw_f1"])
                                P.op("dve", lambda e: e.tensor_tensor(out=f[2][:], in0=f[1][:], in1=f[1][:], op=ALU.mult), reads=["rw_f1"], writes=["rw_f2"])
                                P.op("dve", lambda e: e.scalar_tensor_tensor(out=f[2][:], in0=ps[p2_][:, :128], scalar=1.0 / 64, in1=f[2][:], op0=ALU.mult, op1=ALU.subtract),
                                     reads=[psk[p2_], "rw_f2"], writes=["rw_f2"])
                                P.op("dve", lambda e: e.tensor_scalar(out=f[2][:], in0=f[2][:], scalar1=64e-5, scalar2=None, op0=ALU.add), reads=["rw_f2"], writes=["rw_f2"])
                                P.op("act", lambda e: e.activation(out=f[2][:], in_=f[2][:], func=AF.Sqrt), reads=["rw_f2"], writes=["rw_f2"])
                                P.op("dve", lambda e: e.reciprocal(out=f[2][:], in_=f[2][:]), reads=["rw_f2"], writes=["rw_f2"])
                                P.op("dve", lambda e: e.tensor_tensor(out=f[0][:], in0=ytb[:], in1=f[1][:], op=ALU.subtract), reads=[("rw_y", b), "rw_f1", "rw_f0"], writes=["rw_f0"])
                                P.op("dve", lambda e: e.tensor_tensor(out=f[0][:], in0=f[0][:], in1=f[2][:], op=ALU.mult), reads=["rw_f0", "rw_f2"], writes=["rw_f0"])
                                P.op("act", lambda e: e.activation(out=f[0][:], in_=f[0][:], func=AF.Identity, bias=rp[:, hp, 4:5], scale=rp[:, hp, 3:4]),
                                     reads=["rw_f0", "rw_p2"], writes=["rw_f0"])
                                P.op("dve", lambda e: e.tensor_tensor(out=f[3][:], in0=ld[4][b][:], in1=ps[p3_][:, :128], op=ALU.mult), reads=[("rw_ld", 4, b), psk[p3_]], writes=["rw_f3"])
                                P.op("dve", lambda e: e.tensor_tensor(out=f[0][:], in0=f[0][:], in1=f[3][:], op=ALU.add), reads=["rw_f0", "rw_f3"], writes=["rw_f0"])
                                P.op("dve", lambda e: e.tensor_tensor(out=ytb[:], in0=f[0][:], in1=f[4][:], op=ALU.mult), reads=["rw_f0", "rw_f4", ("rw_y", b)], writes=[("rw_y", b)])
                                P.dma("sp", RWout[cs, tsl], ytb[:], reads=[("rw_y", b)], writes=[("RWout", hp, c)])
                cnt["psmod"] = 8
                P.barrier()
                P.emit()
                with contextlib.ExitStack() as ph3:
                    src = sbp(ph3, "rw_os", [128, T])
                    dst = sbp(ph3, "rw_od", [128, T])
                    for hp in range(4):
                        cs = slice(hp * 128, (hp + 1) * 128)
                        P.dma("sp", src[:], RWout[cs, :], reads=["RWout"], writes=["rw_os"])
                        P.op("act", lambda e: e.copy(out=dst[:, 0:NCTX], in_=src[:, 0:NCTX]), reads=["rw_os"], writes=["rw_od"])
                        P.op("dve", lambda e: e.tensor_copy(out=perm_in(dst[:, LAT]), in_=src[:, LAT].rearrange("p (c r) -> p c r", r=32)),
                             reads=["rw_os"], writes=["rw_od"])
                        P.dma("sp", Y[YR0 + hp * 128:YR0 + (hp + 1) * 128, :], dst[:], reads=["rw_od"], writes=[("Y", YR0 + hp * 128)])
                    P.barrier()
                    P.emit()

        for l in range(NL):
            if "p1" in phases:
                phase_p1(l)
            if "mix" in phases:
                if "conv" in mix:
                    phase_conv(l)
                if "fft" in mix:
                    phase_fft(l)
                if "ssd" in mix:
                    phase_ssd(l)
                if "rwkv" in mix:
                    phase_rwkv(l)
            if "p3" in phases:
                phase_p3(l)
            if "p4" in phases:
                phase_p4(l)

        okeys = []
        for kc in range(KC):
            P.dma("sp", y_out[kc, :, :], xres[kc, :, NCTX:], reads=["xres"] + [("xres", kc, tbi) for tbi in range(5)], writes=[("yo", kc)])
            okeys.append(("yo", kc))
        P.wait_all("sp", okeys)
        P.emit()
    return nc


def dft_consts():
    l = np.arange(2048)
    ang = 2 * np.pi * ((l[:, None] * l[None, :]) % 2048) / 2048.0
    C, S = np.cos(ang).astype(np.float32), np.sin(ang).astype(np.float32)
    c = np.arange(128)
    angd = 2 * np.pi * ((c[:, None] * c[None, :]) % 128) / 128.0
    Cd, nSd = np.cos(angd).astype(np.float32), (-np.sin(angd)).astype(np.float32)
    q = np.arange(256)
    angq = 2 * np.pi * ((q[:, None] * q[None, :]) % 256) / 256.0
    C256 = np.cos(angq).astype(np.float32).reshape(2, 128, 256).transpose(1, 0, 2).reshape(128, 512)
    S256 = np.sin(angq).astype(np.float32).reshape(2, 128, 256).transpose(1, 0, 2).reshape(128, 512)
    small = np.ascontiguousarray(np.concatenate([Cd, nSd, C256, S256], 1))
    return C, S, small


def make_in_maps(inp, cfg):
    LS = cfg["layers"]
    NL = len(LS)
    mix = cfg.get("mix", ("conv", "fft", "ssd", "rwkv"))
    phases = cfg.get("phases", ("p1", "mix", "p3", "p4"))
    maps = []
    ng = np.ascontiguousarray(inp["norm_g"][LS].reshape(NL, 4, KC, 128).transpose(3, 0, 1, 2))
    cs = np.concatenate([inp["c"], inp["c_ctx"][None]], 0)
    cT = np.ascontiguousarray(cs.reshape(5, KC, 128).transpose(2, 1, 0))
    if "fft" in mix:
        C, S, small = dft_consts()
    if "conv" in mix:
        cw = inp["conv_w"][LS]
        cp = np.concatenate([cw, inp["conv_b"][LS][:, None], inp["conv_ln_g"][LS][:, None], inp["conv_ln_b"][LS][:, None]], 1)
        convp = np.ascontiguousarray(cp.reshape(NL, 34, 4, 128).transpose(3, 0, 2, 1))
    need_w = ("p1" in phases) or ("p3" in phases)
    ii = np.arange(128)
    cst128 = np.ascontiguousarray(np.stack([np.eye(128), (ii[:, None] <= ii[None, :]), (ii[:, None] >= ii[None, :])], 1).astype(np.float32))
    if "ssd" in mix:
        sp = np.concatenate([inp["ssd_conv_w"][LS], inp["ssd_conv_b"][LS][:, None]], 1)
        ssdp = np.ascontiguousarray(sp.reshape(NL, 6, 14, 128).transpose(3, 0, 2, 1))
        dtb = np.broadcast_to(inp["ssd_dt_bias"][LS].reshape(NL, 24)[None], (128, NL, 24))
        alog = np.broadcast_to(inp["ssd_A_log"][LS].reshape(NL, 24)[None], (128, NL, 24))
        Dh = np.repeat(inp["ssd_D"][LS], 64, axis=1).reshape(NL, 6, 128).transpose(2, 0, 1)
        sng = inp["ssd_norm_g"][LS].reshape(NL, 6, 128).transpose(2, 0, 1)
        ssdv = np.ascontiguousarray(np.concatenate([dtb, alog, Dh, sng], 2).astype(np.float32))
    if "rwkv" in mix:
        ch = lambda a: a.reshape(NL, 4, 128).transpose(2, 0, 1)
        rwp = np.stack([ch(inp["rwkv_k_k"][LS]), ch(inp["rwkv_k_a"][LS]), ch(inp["rwkv_r_k"][LS].reshape(NL, 512)),
                        ch(inp["rwkv_ln_g"][LS]), ch(inp["rwkv_ln_b"][LS]), ch(inp["rwkv_a0"][LS]),
                        ch(inp["rwkv_w0"][LS][:, 0]), ch(inp["rwkv_w0"][LS][:, 1]), np.zeros((128, NL, 4), np.float32)], -1)
        rwp = np.ascontiguousarray(rwp.astype(np.float32))
        mu = inp["rwkv_mu"][LS]
        rwmu = np.zeros((128, NL, 15), np.float32)
        for j in range(13):
            rwmu[:, :, j] = mu[:, j * 128:(j + 1) * 128].T
        rwmu[:64, :, 13] = mu[:, 1664:1728].T
        rwmu[:, :, 14] = mu[:, 1728:1856].T
        rww = np.zeros((128, NL, 3, 512), np.float32)
        rww[:64, :, 0] = inp["rwkv_w2"][LS][:, 0].transpose(1, 0, 2)
        rww[64:, :, 0] = inp["rwkv_w2"][LS][:, 1].transpose(1, 0, 2)
        rww[:64, :, 1] = inp["rwkv_a2"][LS].transpose(1, 0, 2)
        rww[:, :, 2] = inp["rwkv_g2"][LS].transpose(1, 0, 2)
        blkones = np.zeros((128, 128), np.float32)
        blkones[:64, :64] = 1.0
        blkones[64:, 64:] = 1.0
    for c in range(NCORE):
        b = c % 4
        xb = np.concatenate([inp["ctx"][b], inp["x"][b]], 0)
        m = {}
        m["x_in"] = np.ascontiguousarray(xb.T.reshape(KC, 128, T))
        m["cT"] = cT
        sel = np.zeros((128, 5), np.float32)
        sel[:, b] = 1.0
        m["sel"] = sel
        m["ng"] = ng
        m["modw"] = np.ascontiguousarray(inp["mod_w"][LS][:, :, 1536 * c:1536 * (c + 1)])
        m["modb"] = np.ascontiguousarray(inp["mod_b"][LS][:, 1536 * c:1536 * (c + 1)].reshape(NL, 12, 128).transpose(2, 0, 1))
        if need_w:
            m["w_in"] = np.ascontiguousarray(inp["w_in"][LS][:, 256 * c:256 * (c + 1), :])
        if "p3" in phases:
            for nm, r in (("w_o", 256), ("conv_out", 64), ("ssd_out", 96), ("fourier_out", 64), ("rwkv_out", 64)):
                m[nm] = np.ascontiguousarray(inp[nm][LS][:, r * c:r * (c + 1), :])
        if "p4" in phases:
            m["mlp_up"] = np.ascontiguousarray(inp["mlp_up"][LS][:, 256 * c:256 * (c + 1), :])
            m["mlp_down"] = np.ascontiguousarray(inp["mlp_down"][LS][:, 1024 * c:1024 * (c + 1), :])
        if "fft" in mix:
            m["dftC"] = np.ascontiguousarray(C[256 * c:256 * (c + 1)])
            m["dftS"] = np.ascontiguousarray(S[256 * c:256 * (c + 1)])
            m["dft_small"] = small
        if "conv" in mix:
            m["convp"] = convp
        if "ssd" in mix or "rwkv" in mix:
            m["cst128"] = cst128
        if "ssd" in mix:
            m["ssdp"] = ssdp
            m["ssdv"] = ssdv
        if "rwkv" in mix:
            m["rwp"] = rwp
            m["rwmu"] = rwmu
            m["rww"] = rww
            m["blkones"] = blkones
        maps.append(m)
    return maps


_NC_CACHE = {}


def kernel(**inputs):
    inp = {k: np.asarray(v) for k, v in inputs.items()}
    cfg = {"layers": [0, 1, 2, 3]}
    if "nc" not in _NC_CACHE:
        _NC_CACHE["nc"] = build(cfg)
    nc = _NC_CACHE["nc"]
    maps = make_in_maps(inp, cfg)
    res = run_bass_kernel_spmd(nc, maps, core_ids=list(range(NCORE)))
    out = np.empty((4, NLAT, D), np.float32)
    for b in range(4):
        out[b] = np.asarray(res.results[b]["y_out"], dtype=np.float32).reshape(D, NLAT).T
    return out
```
